# Optimizing a Trainium2 kernel written in Bass

```python
import jax, jax.numpy as jnp
from jax import lax
import numpy as np

D_MODEL = 2048
BATCH = 16
SEQ = 256
DEPTH = 2
DEC_BATCH = 2
DEC_SEQ = 1024
PAST_LEN = 512

GRID_W = 64
N_MIXERS = 2
N_FOURIER_LAYERS = (DEPTH + 1) // 2
N_GLA_LAYERS = DEPTH // 2
N_MOD = 9
D_FF = 5632
FOURIER_GROUPS = 4
FOURIER_GROUP_W = D_MODEL // FOURIER_GROUPS
GLA_HEADS = 4
DK_TOT = D_MODEL // 2
DV_TOT = D_MODEL
HEAD_K = DK_TOT // GLA_HEADS
HEAD_V = DV_TOT // GLA_HEADS
GATE_RANK = 16
GATE_TAU = 16.0
CHUNK = 64
ROPE_PAIRS = HEAD_K // 4
ROPE_BASE = 10000.0
EPS = 1e-6
GLA_IN_COLS = 4 * DK_TOT + 2 * DV_TOT + 2 * GATE_RANK
GLA_SPLITS = [DK_TOT, 2 * DK_TOT, 3 * DK_TOT, 4 * DK_TOT, 4 * DK_TOT + DV_TOT,
              4 * DK_TOT + 2 * DV_TOT, 4 * DK_TOT + 2 * DV_TOT + GATE_RANK]

kernel_name = 'fourier_gla_macaron_diffusion_step'


def rms_norm(x, g):
    xf = x.astype(jnp.float32)
    y = xf * lax.rsqrt(jnp.mean(xf * xf, axis=-1, keepdims=True) + EPS)
    return (y * g.astype(jnp.float32)).astype(x.dtype)


def modulation(cond, w, b):
    m = jax.nn.silu(cond) @ w + b
    return m.reshape(cond.shape[0], N_MOD, D_MODEL)


def pre_mod(x, g, m, k):
    return rms_norm(x, g) * (1 + m[:, 3 * k + 1, None, :]) + m[:, 3 * k, None, :]


def post_add(x, out, g, m, k, w):
    return x + w * m[:, 3 * k + 2, None, :] * rms_norm(out, g)


def swiglu(h, w_gu, w_down):
    gate, up = jnp.split(h @ w_gu, 2, axis=-1)
    return (jax.nn.silu(gate) * up) @ w_down


def fourier_mix(h, w):
    B, T, _ = h.shape
    hg = h.astype(jnp.float32).reshape(B, T, FOURIER_GROUPS, FOURIER_GROUP_W)
    f = jnp.fft.fft2(hg, axes=(1, 3), norm='ortho').real
    return f.reshape(B, T, D_MODEL).astype(h.dtype) @ w


def axial_rope(rows):
    row = jnp.repeat(jnp.arange(rows), GRID_W).astype(jnp.float32)
    col = jnp.tile(jnp.arange(GRID_W), rows).astype(jnp.float32)
    inv = ROPE_BASE ** (-jnp.arange(ROPE_PAIRS, dtype=jnp.float32) / ROPE_PAIRS)
    ang = jnp.concatenate([row[:, None] * inv, col[:, None] * inv], axis=-1)
    return jnp.cos(ang), jnp.sin(ang)


def apply_rope(x, rope):
    cos, sin = rope
    c = cos[None, :, None, :]
    s = sin[None, :, None, :]
    xr = x.astype(jnp.float32).reshape(x.shape[:-1] + (HEAD_K // 2, 2))
    xe, xo = xr[..., 0], xr[..., 1]
    out = jnp.stack([xe * c - xo * s, xe * s + xo * c], axis=-1)
    return out.reshape(x.shape).astype(x.dtype)


def gla_scan(q, k, v, g, s0):
    B, T, H, _ = q.shape
    n = T // CHUNK

    def chunks(a):
        return a.astype(jnp.float32).reshape(B, n, CHUNK, H, a.shape[-1]).transpose(1, 0, 3, 2, 4)

    causal = jnp.tril(jnp.ones((CHUNK, CHUNK), dtype=bool))[:, :, None]

    def step(S, inp):
        qc, kc, vc, gc = inp
        b = jnp.cumsum(gc, axis=2)
        o_inter = jnp.einsum('bhik,bhkv->bhiv', qc * jnp.exp(b), S)
        diff = b[:, :, :, None, :] - b[:, :, None, :, :]
        decay = jnp.exp(jnp.where(causal, diff, -jnp.inf))
        scores = jnp.einsum('bhik,bhjk,bhijk->bhij', qc, kc, decay)
        o = o_inter + jnp.einsum('bhij,bhjv->bhiv', scores, vc)
        b_last = b[:, :, -1:, :]
        S = jnp.exp(b_last[:, :, 0, :, None]) * S + jnp.einsum(
            'bhjk,bhjv->bhkv', kc * jnp.exp(b_last - b), vc)
        return S, o

    S, o = lax.scan(step, s0.astype(jnp.float32), (chunks(q), chunks(k), chunks(v), chunks(g)))
    o = o.transpose(1, 0, 3, 2, 4).reshape(B, T, H, v.shape[-1])
    return o.astype(v.dtype), S.astype(v.dtype)


def gla_mix(h, w_in, wg_up, b_g, g_norm, w_out, s0, rope):
    B, T, _ = h.shape
    qf, kf, qb, kb, v, r, lr_f, lr_b = jnp.split(h @ w_in, GLA_SPLITS, axis=-1)

    def heads(a):
        return a.reshape(B, T, GLA_HEADS, -1)

    qf, kf, qb, kb, v, r = (heads(a) for a in (qf, kf, qb, kb, v, r))
    if rope is not None:
        qf, kf, qb, kb = (apply_rope(a, rope) for a in (qf, kf, qb, kb))

    def log_decay(lr, d):
        return heads(jax.nn.log_sigmoid((lr @ wg_up[d] + b_g[d]).astype(jnp.float32)) / GATE_TAU)

    scale = HEAD_K ** -0.5
    o_f, s_f = gla_scan(qf * scale, kf, v, log_decay(lr_f, 0), s0[:, 0])

    def flip(a):
        return jnp.flip(a, axis=1)

    o_b, s_b = gla_scan(flip(qb) * scale, flip(kb), flip(v), flip(log_decay(lr_b, 1)), s0[:, 1])
    o = rms_norm(o_f + flip(o_b), g_norm) * jax.nn.silu(r)
    return o.reshape(B, T, DV_TOT) @ w_out, jnp.stack([s_f, s_b], axis=1)


def trunk(x, cond, gla_init, rope, ada_w, ada_b, norm_pre, norm_post, ffn_w_gate_up,
          ffn_w_down, fourier_w, gla_w_in, gla_w_gate_up, gla_b_gate, gla_norm, gla_w_out):
    B = x.shape[0]
    states = []
    for l in range(DEPTH):
        m = modulation(cond, ada_w[l], ada_b[l])
        h = pre_mod(x, norm_pre[l, 0], m, 0)
        x = post_add(x, swiglu(h, ffn_w_gate_up[l, 0], ffn_w_down[l, 0]), norm_post[l, 0], m, 0, 0.5)
        h = pre_mod(x, norm_pre[l, 1], m, 1)
        j = l // N_MIXERS
        if l % N_MIXERS == 0:
            out = fourier_mix(h, fourier_w[j])
        else:
            if gla_init is None:
                s0 = jnp.zeros((B, 2, GLA_HEADS, HEAD_K, HEAD_V), x.dtype)
            else:
                s0 = gla_init[:, j]
            out, st = gla_mix(h, gla_w_in[j], gla_w_gate_up[j], gla_b_gate[j], gla_norm[j],
                              gla_w_out[j], s0, rope)
            states.append(st)
        x = post_add(x, out, norm_post[l, 1], m, 1, 1.0)
        h = pre_mod(x, norm_pre[l, 2], m, 2)
        x = post_add(x, swiglu(h, ffn_w_gate_up[l, 1], ffn_w_down[l, 1]), norm_post[l, 2], m, 2, 0.5)
    return x, states


def setup_inputs(seed: int = 0) -> dict:
    key = jax.random.key(seed)
    ks = jax.random.split(key, 17)
    f32 = jnp.float32
    nrm = lambda k, s, sc: jax.random.normal(k, s, f32) * sc
    return {
        'x_prompt': nrm(ks[0], (BATCH, SEQ, D_MODEL), 1.0),
        'x_sample': nrm(ks[1], (DEC_BATCH, DEC_SEQ, D_MODEL), 1.0),
        'state_gla': nrm(ks[2], (DEC_BATCH, N_GLA_LAYERS, 2, GLA_HEADS, HEAD_K, HEAD_V), 0.5),
        'c': nrm(ks[3], (DEC_BATCH, D_MODEL), 1.0),
        'c_ctx': nrm(ks[4], (D_MODEL,), 1.0),
        'ada_w': nrm(ks[5], (DEPTH, D_MODEL, N_MOD * D_MODEL), 0.5 * D_MODEL ** -0.5),
        'ada_b': nrm(ks[6], (DEPTH, N_MOD * D_MODEL), 0.02),
        'norm_pre': 1.0 + nrm(ks[7], (DEPTH, 3, D_MODEL), 0.02),
        'norm_post': 1.0 + nrm(ks[8], (DEPTH, 3, D_MODEL), 0.02),
        'ffn_w_gate_up': nrm(ks[9], (DEPTH, 2, D_MODEL, 2 * D_FF), D_MODEL ** -0.5),
        'ffn_w_down': nrm(ks[10], (DEPTH, 2, D_FF, D_MODEL), D_FF ** -0.5),
        'fourier_w': nrm(ks[11], (N_FOURIER_LAYERS, D_MODEL, D_MODEL), D_MODEL ** -0.5),
        'gla_w_in': nrm(ks[12], (N_GLA_LAYERS, D_MODEL, GLA_IN_COLS), D_MODEL ** -0.5),
        'gla_w_gate_up': nrm(ks[13], (N_GLA_LAYERS, 2, GATE_RANK, DK_TOT), GATE_RANK ** -0.5),
        'gla_b_gate': nrm(ks[14], (N_GLA_LAYERS, 2, DK_TOT), 0.1),
        'gla_norm': 1.0 + nrm(ks[15], (N_GLA_LAYERS, HEAD_V), 0.02),
        'gla_w_out': nrm(ks[16], (N_GLA_LAYERS, DV_TOT, D_MODEL), DV_TOT ** -0.5),
    }


def reference(x_prompt, x_sample, state_gla, c, c_ctx, ada_w, ada_b, norm_pre, norm_post,
              ffn_w_gate_up, ffn_w_down, fourier_w, gla_w_in, gla_w_gate_up, gla_b_gate,
              gla_norm, gla_w_out):
    y_prompt, ctx_states = trunk(x_prompt, c_ctx[None, :], None, None, ada_w, ada_b, norm_pre,
                                 norm_post, ffn_w_gate_up, ffn_w_down, fourier_w, gla_w_in,
                                 gla_w_gate_up, gla_b_gate, gla_norm, gla_w_out)
    new_state_gla = jnp.stack(ctx_states, axis=1)
    rows = x_sample.shape[1] // GRID_W
    rope = axial_rope(rows)
    y_sample, _ = trunk(x_sample, c, state_gla, rope, ada_w, ada_b, norm_pre, norm_post,
                        ffn_w_gate_up, ffn_w_down, fourier_w, gla_w_in, gla_w_gate_up,
                        gla_b_gate, gla_norm, gla_w_out)
    return (y_prompt, y_sample, new_state_gla)
```

```python
import numpy as np
import concourse.bass as bass
import concourse.mybir as mybir
from concourse.bass_utils import run_bass_kernel_spmd

F32 = mybir.dt.float32
BF16 = mybir.dt.bfloat16
AF = mybir.ActivationFunctionType
ALU = mybir.AluOpType

D = 2048
T = 1024
NC_ = 16
DFF = 5632
EPS = 1e-6
NCORES = 6
NSLOT = 6
SLOT = 4096


class Res:
    __slots__ = ("w", "r", "name")

    def __init__(self, name=""):
        self.w = {}
        self.r = {}
        self.name = name


class Prog:
    COMPUTE = ("pe", "act", "dve", "pool")

    def __init__(self, nc):
        self.nc = nc
        self.streams = {e: [] for e in ("pe", "act", "dve", "pool", "sp")}
        self.sem = {}
        self.count = {e: 0 for e in self.COMPUTE}
        self.known = {e: {} for e in self.streams}
        for e in self.COMPUTE:
            self.sem[e] = nc.alloc_semaphore(name="prog_" + e)
        self.dsem = {"sp": [nc.alloc_semaphore(name=f"dsp{i}") for i in range(8)],
                     "pool": [nc.alloc_semaphore(name=f"dpl{i}") for i in range(8)]}
        self.dcnt = {q: [0] * len(v) for q, v in self.dsem.items()}
        self.drr = {q: 0 for q in self.dsem}
        self.semobj = {}
        for e in self.COMPUTE:
            self.semobj[id(self.sem[e])] = self.sem[e]
        for q in self.dsem:
            for s in self.dsem[q]:
                self.semobj[id(s)] = s
        self.out_tokens = []

    def _collect(self, eng, reads, writes, own):
        need = {}
        war = {}
        for r in reads:
            for k, v in r.w.items():
                if need.get(k, 0) < v:
                    need[k] = v
        for w in writes:
            for k, v in w.w.items():
                if need.get(k, 0) < v:
                    need[k] = v
            for k, v in w.r.items():
                if war.get(k, 0) < v:
                    war[k] = v
        final = []
        kn = self.known[eng]
        for k, v in need.items():
            if own is not None and k == own:
                if eng == "pe":
                    continue
                if v <= self.count[eng] - 2:
                    continue
            if kn.get(k, 0) >= v:
                continue
            kn[k] = v
            final.append((k, v))
        for k, v in war.items():
            if own is not None and k == own:
                if eng == "pe" or v <= self.count[eng] - 2:
                    continue
            if kn.get(k, 0) >= v:
                continue
            kn[k] = v
            final.append((k, v))
        return final

    def _mark(self, reads, writes, key, val):
        for r in reads:
            if r.r.get(key, 0) < val:
                r.r[key] = val
        for w in writes:
            w.r.clear()
            if w.w.get(key, 0) < val:
                w.w[key] = val

    def op(self, eng, fn, reads=(), writes=(), inc=True):
        own = id(self.sem[eng])
        final = self._collect(eng, reads, writes, own)
        val = self.count[eng] + 1
        if inc:
            self.count[eng] = val
        self._mark(reads, writes, own, val)
        self.streams[eng].append((final, fn, ("c", own) if inc else None))

    def dma(self, q, fn, reads=(), writes=(), is_output=False):
        i = self.drr[q]
        self.drr[q] = (i + 1) % len(self.dsem[q])
        s = self.dsem[q][i]
        key = id(s)
        own = id(self.sem[q]) if q in self.sem else None
        final = self._collect(q, reads, writes, None)
        prev = self.dcnt[q][i]
        if prev > 0 and self.known[q].get(key, 0) < 16 * prev:
            self.known[q][key] = 16 * prev
            final.append((key, 16 * prev))
        self.dcnt[q][i] = prev + 1
        val = 16 * (prev + 1)
        self._mark(reads, writes, key, val)
        self.streams[q].append((final, fn, ("d", key)))
        if is_output:
            self.out_tokens.append((key, val))

    def barrier(self):
        targets = {}
        for e in self.COMPUTE:
            if self.count[e] > 0:
                targets[id(self.sem[e])] = self.count[e]
        for q in self.dsem:
            for i, s in enumerate(self.dsem[q]):
                if self.dcnt[q][i] > 0:
                    targets[id(s)] = 16 * self.dcnt[q][i]
        for e in self.streams:
            own = id(self.sem[e]) if e in self.sem else None
            waits = []
            for k, v in targets.items():
                if k == own:
                    continue
                if self.known[e].get(k, 0) >= v:
                    continue
                self.known[e][k] = v
                waits.append((k, v))
            if waits:
                self.streams[e].append((waits, None, None))

    def check_deadlock(self):
        sems = {}
        pos = {e: 0 for e in self.streams}
        n = {e: len(s) for e, s in self.streams.items()}
        progress = True
        while progress:
            progress = False
            for e, st in self.streams.items():
                while pos[e] < n[e]:
                    waits, fn, inc = st[pos[e]]
                    if all(sems.get(k, 0) >= v for k, v in waits):
                        if inc is not None:
                            sems[inc[1]] = sems.get(inc[1], 0) + (1 if inc[0] == "c" else 16)
                        pos[e] += 1
                        progress = True
                    else:
                        break
        stuck = {e: (pos[e], n[e]) for e in pos if pos[e] < n[e]}
        if stuck:
            msg = []
            for e, (p, _) in stuck.items():
                waits = self.streams[e][p][0]
                msg.append((e, p, [(self.semobj[k].name if hasattr(self.semobj[k], "name") else k, v, sems.get(k, 0)) for k, v in waits]))
            raise RuntimeError(f"DEADLOCK: {stuck} {msg}")
        return {e: n[e] for e in n}

    def emit(self):
        nc = self.nc
        fin = {}
        for k, v in self.out_tokens:
            fin[k] = max(fin.get(k, 0), v)
        semobj = self.semobj
        streams = self.streams

        def run(e, name):
            for waits, fn, inc in streams[name]:
                for k, v in waits:
                    e.wait_ge(semobj[k], v)
                if fn is None:
                    continue
                ins = fn(e)
                if inc is not None:
                    if inc[0] == "c":
                        ins.then_inc(semobj[inc[1]], 1)
                    else:
                        ins.then_inc(semobj[inc[1]], 16)
            if name == "sp":
                for k, v in fin.items():
                    e.wait_ge(semobj[k], v)

        with nc.Block() as block:
            @block.tensor
            def _(e):
                run(e, "pe")

            @block.scalar
            def _(e):
                run(e, "act")

            @block.vector
            def _(e):
                run(e, "dve")

            @block.gpsimd
            def _(e):
                run(e, "pool")

            @block.sync
            def _(e):
                run(e, "sp")


class Builder:
    def __init__(self, stop_after=None):
        self.stop_after = stop_after
        nc = self.nc = bass.Bass("TRN2", target_bir_lowering=False)
        self.P = Prog(nc)
        P = self.P
        dt = nc.dram_tensor
        self.xT = dt("xT", [D, T], F32, kind="ExternalInput").ap()
        self.cond = dt("cond", [128, 16], F32, kind="ExternalInput").ap()
        self.ada_w = dt("ada_w", [2, D, 9 * D], F32, kind="ExternalInput").ap()
        self.ada_bT = dt("ada_bT", [128, 2 * 144], F32, kind="ExternalInput").ap()
        self.nvec = dt("nvec", [128, 193], F32, kind="ExternalInput").ap()
        self.ffn_gu = dt("ffn_gu", [2, 2, D, 2 * DFF], F32, kind="ExternalInput").ap()
        self.ffn_dn = dt("ffn_dn", [2, 2, DFF, D], F32, kind="ExternalInput").ap()
        self.ccsc = dt("ccsc", [2, 512, 512], F32, kind="ExternalInput").ap()
        self.ctst = dt("ctst", [2, T, T], F32, kind="ExternalInput").ap()
        self.fw = dt("fw", [D, D], F32, kind="ExternalInput").ap()
        self.win = dt("win", [D, 8224], F32, kind="ExternalInput").ap()
        self.wg = dt("wg", [33, 2048], F32, kind="ExternalInput").ap()
        self.gnorm = dt("gnorm", [128, 512], F32, kind="ExternalInput").ap()
        self.wout = dt("wout", [D, D], F32, kind="ExternalInput").ap()
        self.ropecs = dt("ropecs", [2, 128, T], F32, kind="ExternalInput").ap()
        self.sinit = dt("sinit", [2, 4, 2, 128, 512], F32, kind="ExternalInput").ap()
        self.flags = dt("flags", [128, 24], F32, kind="ExternalInput").ap()
        self.gconst = dt("gconst", [128, 1408], F32, kind="ExternalInput").ap()
        self.st_out = dt("st_out", [2, 4, 4, 2, 128, 512], F32, kind="ExternalOutput").ap()
        self.yT = dt("yT", [D, T], F32, kind="ExternalOutput").ap()
        self.xd = dt("xd", [D, T], F32, kind="Internal").ap()
        self.xd_res = [Res(f"xd{c}") for c in range(NC_)]

        A = nc.alloc_sbuf_tensor
        self.hT = A("hT", [128, NC_, T], BF16)
        self.hT_res = Res("hT")
        self.ob = A("ob", [128, NC_, T], F32)
        self.ob_res = [Res(f"ob{c}") for c in range(NC_)]
        self.ring = A("ring", [128, NSLOT, SLOT], BF16)
        self.ring_res = [Res(f"ring{i}") for i in range(NSLOT)]
        self.ring_next = 0
        self.ring_nslot = NSLOT
        self.U = A("U", [128, 8704], F32)
        self.hid = self.U[:, 0:4096].bitcast(BF16).rearrange("p (a b t) -> p a b t", a=2, b=4)
        self.hid_res = [Res("hid0"), Res("hid1")]
        self.sg = self.U[:, 4096:4864].bitcast(BF16).rearrange("p (a n) -> p a n", a=3)
        self.sg_res = [Res(f"sg{i}") for i in range(3)]
        self.sg_next = 0
        self.xin = A("xin", [128, 2, T], F32)
        self.xin_res = [Res("xin0"), Res("xin1")]
        self.sq = A("sq", [128, 2, T], BF16)
        self.sq_res = [Res("sq0"), Res("sq1")]
        self.sq_next = 0
        self.sqb = [self.sq[:, 0, :], self.sq[:, 1, :], self.U[:, 6912:7424].bitcast(BF16)]
        self.sqb_res = [self.sq_res[0], self.sq_res[1], Res("sq2")]
        self.stats_pending = []
        self.xb = [self.xin[:, 0, :], self.xin[:, 1, :], self.U[:, 4864:5888], self.U[:, 5888:6912]]
        self.xb_res = [self.xin_res[0], self.xin_res[1], Res("xb2"), Res("xb3")]
        self.rstd = A("rstd", [128, T], F32)
        self.rstd_res = Res("rstd")
        self.tmp = A("tmp", [128, 2, T], F32)
        self.tmp_res = [Res("tmp0"), Res("tmp1")]
        self.tmp_next = 0
        self.ones = A("ones", [128, 128], BF16)
        self.ones_res = Res("ones")
        self.onef = A("onef", [128, 2], F32)
        self.onef_res = Res("onef")
        self.epsb = A("epsb", [128, 1], F32)
        self.cond_sb = A("cond_sb", [128, 16], F32)
        self.cond_res = Res("cond")
        self.s_bf = A("s_bf", [128, 16], BF16)
        self.s_res = Res("s_bf")
        self.adab_sb = A("adab_sb", [128, 288], F32)
        self.adab_res = Res("adab")
        self.nvec_sb = A("nvec_sb", [128, 193], F32)
        self.nvec_res = Res("nvec")
        self.modrow = self.U[0:65, 7680:8704].rearrange("p (a n) -> p a n", a=2)
        self.modrow_res = [Res("modrow0"), Res("modrow1")]
        self.modT = A("modT", [128, 2, 144], F32)
        self.modT_res = [[Res(f"modT{l}_{m}") for m in range(9)] for l in range(2)]
        def spread(total, n=11):
            return [(total * (g + 1)) // n - (total * g) // n for g in range(n)]
        self.mod_rate = {(0, 0): spread(24), (0, 1): spread(24), (1, 0): spread(16)}
        self.modrow_next = 0
        self.mod_queue = [(l, blk) for l in range(2) for blk in range(36)]
        self.coef = A("coef", [128, 2, 9, 16], F32)
        self.coef_res = [Res("coef0"), Res("coef1")]
        self.g_flags = A("g_flags", [128, 24], F32)
        self.g_decp = A("g_decp", [128, 2, 8], F32)
        self.g_hstat = A("g_hstat", [128, 16], F32)
        self.g_decay = A("g_decay", [128, 2, 8], F32)
        self.g_ltot = A("g_ltot", [128, 2, 8], F32)
        self.g_ident = A("g_ident", [128, 128], BF16)
        self.ps = nc.alloc_psum_tensor("ps", [128, 8, 512], F32)
        self.ps_res = [Res(f"ps{i}") for i in range(8)]
        self.proj_next = 0
        self.down_next = 0

    def ring_load(self, src_ap, kc, n):
        P = self.P
        i = self.ring_next % self.ring_nslot
        self.ring_next += 1
        assert kc * n <= SLOT
        view = self.ring[:, i, 0:kc * n].rearrange("p (c n) -> p c n", c=kc)
        res = self.ring_res[i]
        src = src_ap.rearrange("(c p) n -> p c n", p=128)
        P.dma("pool", lambda e, o=view, s=src: e.dma_start(out=o, in_=s), reads=(), writes=(res,))
        return view, res

    def proj_bank(self):
        b = self.proj_next % getattr(self, "proj_nbanks", 4)
        self.proj_next += 1
        return b

    def down_bank(self):
        b = 4 + self.down_next % getattr(self, "down_nbanks", 2)
        self.down_next += 1
        return b

    def mm(self, out, lhsT, rhs, start, stop, reads, writes, inc):
        self.P.op("pe", lambda e: e.matmul(out, lhsT, rhs, start=start, stop=stop), reads, writes, inc)

    def act(self, out, in_, func, reads, writes, bias=None, scale=None):
        kw = {}
        if bias is not None:
            kw["bias"] = bias
        if scale is not None:
            kw["scale"] = scale
        self.P.op("act", lambda e: e.activation(out, in_, func, **kw), reads, writes)

    def tt(self, out, in0, in1, op, reads, writes, eng="dve"):
        self.P.op(eng, lambda e: e.tensor_tensor(out, in0, in1, op), reads, writes)

    def stt(self, out, in0, scalar, in1, op0, op1, reads, writes):
        self.P.op("dve", lambda e: e.scalar_tensor_tensor(out, in0, scalar, in1, op0, op1), reads, writes)

    def setup(self):
        P = self.P
        P.op("dve", lambda e: e.memset(self.ps[:].rearrange("p b n -> p (b n)"), 0.0), (), tuple(self.ps_res))
        P.op("dve", lambda e: e.memset(self.ones[:], 1.0), (), (self.ones_res,))
        P.op("dve", lambda e: e.memset(self.onef[:], 1.0), (), (self.onef_res,))
        P.op("dve", lambda e: e.memset(self.epsb[:], EPS), (), (self.onef_res,))
        P.dma("sp", lambda e: e.dma_start(out=self.cond_sb[:], in_=self.cond), (), (self.cond_res,))
        P.dma("sp", lambda e: e.dma_start(out=self.adab_sb[:], in_=self.ada_bT), (), (self.adab_res,))
        P.dma("sp", lambda e: e.dma_start(out=self.nvec_sb[:], in_=self.nvec), (), (self.nvec_res,))
        self.act(self.s_bf[:], self.cond_sb[:], AF.Silu, (self.cond_res,), (self.s_res,))

    def mod_block(self, l, blk):
        W = self.ada_w[l]
        n0 = blk * 512
        rb = self.proj_bank()
        tiles = [self.ring_load(W[half * 1024:(half + 1) * 1024, n0:n0 + 512], 8, 512) for half in range(2)]
        groups = [[kc for kc in range(16) if kc % 3 == j] for j in range(3)]
        seq = [(j, r) for r in range(6) for j in range(3) if r < len(groups[j])]
        for idx, (j, r) in enumerate(seq):
            kc = groups[j][r]
            wt, wres = tiles[kc // 8]
            self.P.op("pe", lambda e, o=self.ps[32 * j:32 * j + 1, rb, :], a=self.s_bf[:, kc:kc + 1], b=wt[:, kc % 8, :],
                      st=(r == 0), sp=(r == len(groups[j]) - 1), tp=(0, 32 * j): e.matmul(o, a, b, start=st, stop=sp, tile_position=tp),
                      (wres, self.s_res), (self.ps_res[rb],), idx == len(seq) - 1)
        mr = self.modrow_next % 2
        self.modrow_next += 1
        self.P.op("dve", lambda e, o=self.modrow[0:65, mr, :], i=self.ps[0:65, rb, :]: e.tensor_copy(o, i),
                  (self.ps_res[rb],), (self.modrow_res[mr],))
        cb = self.proj_bank()
        for j in range(4):
            self.mm(self.ps[:, cb, j:j + 1], self.modrow[0:65, mr, j * 128:(j + 1) * 128], self.nvec_sb[0:65, 192:193],
                    True, True, (self.modrow_res[mr], self.nvec_res), (self.ps_res[cb],), j == 3)
        m = blk // 4
        self.tt(self.modT[:, l, blk * 4:blk * 4 + 4], self.ps[:, cb, 0:4], self.adab_sb[:, l * 144 + blk * 4:l * 144 + blk * 4 + 4], ALU.add,
                (self.ps_res[cb], self.adab_res), (self.modT_res[l][m],))

    def mod_pump(self, n):
        while n > 0 and self.mod_queue:
            l, blk = self.mod_queue.pop(0)
            self.mod_block(l, blk)
            n -= 1

    def mod_require(self, l, m):
        while any(ql == l and qb // 4 == m for ql, qb in self.mod_queue):
            self.mod_pump(1)

    def coef_AB(self, l, k):
        self.mod_require(l, 3 * k)
        self.mod_require(l, 3 * k + 1)
        shift = self.modT[:, l, (3 * k) * 16:(3 * k) * 16 + 16]
        scale = self.modT[:, l, (3 * k + 1) * 16:(3 * k + 1) * 16 + 16]
        gpre = self.nvec_sb[:, ((l * 3 + k) * 2 + 0) * 16:((l * 3 + k) * 2 + 0) * 16 + 16]
        cA = self.coef[:, l, 3 * k + 0, :]
        cB = self.coef[:, l, 3 * k + 1, :]
        self.stt(cA, scale, 1.0, gpre, ALU.add, ALU.mult, (self.modT_res[l][3 * k + 1], self.nvec_res), (self.coef_res[l],))
        self.P.op("dve", lambda e, o=cB, i=shift: e.tensor_copy(o, i), (self.modT_res[l][3 * k],), (self.coef_res[l],))

    def coef_C(self, l, k):
        self.mod_require(l, 3 * k + 2)
        gate = self.modT[:, l, (3 * k + 2) * 16:(3 * k + 2) * 16 + 16]
        gpost = self.nvec_sb[:, ((l * 3 + k) * 2 + 1) * 16:((l * 3 + k) * 2 + 1) * 16 + 16]
        w = 1.0 if k == 1 else 0.5
        cC = self.coef[:, l, 3 * k + 2, :]
        self.stt(cC, gate, w, gpost, ALU.mult, ALU.mult, (self.modT_res[l][3 * k + 2], self.nvec_res), (self.coef_res[l],))

    def stats_acc(self, c, defer=2):
        si = self.sq_next % 3
        self.sq_next += 1
        self.act(self.sqb[si], self.ob[:, c, :], AF.Square, (self.ob_res[c],), (self.sqb_res[si],))
        self.stats_pending.append((c, si))
        while len(self.stats_pending) > defer:
            self._stats_mm(*self.stats_pending.pop(0))

    def _stats_mm(self, c, si):
        banks = (6, 7)
        for t in range(2):
            self.mm(self.ps[:, banks[t], :], self.ones[:], self.sqb[si][:, t * 512:(t + 1) * 512],
                    c == 0, c == NC_ - 1, (self.sqb_res[si], self.ones_res), (self.ps_res[banks[t]],), t == 1)

    def stats_finish(self):
        while self.stats_pending:
            self._stats_mm(*self.stats_pending.pop(0))
        banks = (6, 7)
        tis = []
        for t in range(2):
            ti = self.tmp_next % 2
            self.tmp_next += 1
            tis.append(ti)
            self.act(self.tmp[:, ti, 0:512], self.ps[:, banks[t], :], AF.Ln, (self.ps_res[banks[t]], self.onef_res),
                     (self.tmp_res[ti],), bias=self.epsb[:, 0:1], scale=1.0 / D)
        for t in range(2):
            ti = tis[t]
            self.act(self.rstd[:, t * 512:(t + 1) * 512], self.tmp[:, ti, 0:512], AF.Exp, (self.tmp_res[ti],), (self.rstd_res,),
                     scale=-0.5)

    def load_x0(self):
        P = self.P
        for c in range(NC_):
            P.dma("sp", lambda e, o=self.ob[:, c, :], i=self.xT[c * 128:(c + 1) * 128, :]: e.dma_start(out=o, in_=i),
                  (), (self.ob_res[c],))
            self.stats_acc(c)

    def residual(self, l, k, x_src, x_src_res, x_dst, x_dst_res, is_output=False):
        P = self.P
        NB = 4

        def load(c):
            bi = c % NB
            P.dma("sp", lambda e, o=self.xb[bi], i=x_src[c * 128:(c + 1) * 128, :]: e.dma_start(out=o, in_=i),
                  (x_src_res[c],) if x_src_res else (), (self.xb_res[bi],))

        def store(c):
            P.dma("sp", lambda e, o=x_dst[c * 128:(c + 1) * 128, :], i=self.ob[:, c, :]: e.dma_start(out=o, in_=i),
                  (self.ob_res[c],), (x_dst_res[c],) if x_dst_res else (), is_output=is_output)

        for c in range(NB):
            load(c)
        self.stats_finish()
        cC = self.coef[:, l, 3 * k + 2, :]

        def scale(c):
            self.stt(self.ob[:, c, :], self.ob[:, c, :], cC[:, c:c + 1], self.rstd[:], ALU.mult, ALU.mult,
                     (self.ob_res[c], self.rstd_res, self.coef_res[l]), (self.ob_res[c],))

        scale(0)
        for c in range(NC_):
            if c + 1 < NC_:
                scale(c + 1)
            bi = c % NB
            self.tt(self.ob[:, c, :], self.ob[:, c, :], self.xb[bi], ALU.add, (self.ob_res[c], self.xb_res[bi]), (self.ob_res[c],))
            store(c)
            if c + NB < NC_:
                load(c + NB)
            if not is_output:
                self.stats_acc(c)

    def prenorm(self, l, k):
        self.stats_finish()
        cA = self.coef[:, l, 3 * k + 0, :]
        cB = self.coef[:, l, 3 * k + 1, :]
        for c in range(NC_):
            ti = self.tmp_next % 2
            self.tmp_next += 1
            self.tt(self.tmp[:, ti, :], self.ob[:, c, :], self.rstd[:], ALU.mult,
                    (self.ob_res[c], self.rstd_res), (self.tmp_res[ti],))
            self.act(self.hT[:, c, :], self.tmp[:, ti, :], AF.Identity, (self.tmp_res[ti], self.coef_res[l]),
                     (self.hT_res,), bias=cB[:, c:c + 1], scale=cA[:, c:c + 1])

    def ffn(self, l, i):
        P = self.P
        wgu = self.ffn_gu[l, i]
        wdn = self.ffn_dn[l, i]
        NG = DFF // 512

        def gate_up(grp):
            hb = grp % 2
            f0 = grp * 512
            for half in range(2):
                fa = f0 + half * 256
                gt, gres = self.ring_load(wgu[:, fa:fa + 256], 16, 256)
                ut, ures = self.ring_load(wgu[:, DFF + fa:DFF + fa + 256], 16, 256)
                for j in range(2):
                    ffc = half * 2 + j
                    for t in range(2):
                        b1 = self.proj_bank()
                        for kc in range(16):
                            self.mm(self.ps[:, b1, :], gt[:, kc, j * 128:(j + 1) * 128], self.hT[:, kc, t * 512:(t + 1) * 512],
                                    kc == 0, kc == 15, (gres, self.hT_res), (self.ps_res[b1],), kc == 15)
                        si = self.sg_next % 3
                        self.sg_next += 1
                        self.act(self.sg[:, si, :], self.ps[:, b1, :], AF.Silu, (self.ps_res[b1],), (self.sg_res[si],))
                        b2 = self.proj_bank()
                        for kc in range(16):
                            self.mm(self.ps[:, b2, :], ut[:, kc, j * 128:(j + 1) * 128], self.hT[:, kc, t * 512:(t + 1) * 512],
                                    kc == 0, kc == 15, (ures, self.hT_res), (self.ps_res[b2],), kc == 15)
                        self.tt(self.hid[:, hb, ffc, t * 512:(t + 1) * 512], self.ps[:, b2, :], self.sg[:, si, :], ALU.mult,
                                (self.ps_res[b2], self.sg_res[si]), (self.hid_res[hb],))

        def down(grp):
            self.down_nbanks = 2 if grp == NG - 1 else 4
            hb = grp % 2
            f0 = grp * 512
            dA, dAres = self.ring_load(wdn[f0:f0 + 256, :], 2, D)
            dB, dBres = self.ring_load(wdn[f0 + 256:f0 + 512, :], 2, D)
            for n in range(NC_):
                for t in range(2):
                    b = self.down_bank()
                    for ffc in range(4):
                        wt, wres = (dA, dAres) if ffc < 2 else (dB, dBres)
                        self.mm(self.ps[:, b, :], wt[:, ffc % 2, n * 128:(n + 1) * 128], self.hid[:, hb, ffc, t * 512:(t + 1) * 512],
                                ffc == 0, ffc == 3, (wres, self.hid_res[hb]), (self.ps_res[b],), ffc == 3)
                    o = self.ob[:, n, t * 512:(t + 1) * 512]
                    if grp == 0:
                        self.P.op("act", lambda e, o=o, i=self.ps[:, b, :]: e.copy(o, i), (self.ps_res[b],), (self.ob_res[n],))
                    else:
                        self.tt(o, self.ps[:, b, :], o, ALU.add, (self.ps_res[b], self.ob_res[n]), (self.ob_res[n],))
                if grp == NG - 1:
                    self.stats_acc(n)

        gate_up(0)
        for grp in range(NG):
            if grp + 1 < NG:
                gate_up(grp + 1)
            self.mod_pump(self.mod_rate[(l, i)][grp] if (l, i) in self.mod_rate else 0)
            down(grp)

    def evac(self, out, in_, reads, writes):
        self.evac_next = getattr(self, "evac_next", 0) + 1
        if self.evac_next % 2:
            self.P.op("act", lambda e: e.copy(out, in_), reads, writes)
        else:
            self.P.op("dve", lambda e: e.tensor_copy(out, in_), reads, writes)

    def fourier(self):
        P = self.P
        P.barrier()
        ccv = self.U[:, 0:2048].bitcast(BF16).rearrange("p (a c n) -> p a c n", a=2, c=4)
        cc_res = self.hid_res[0]
        for cs in range(2):
            P.dma("pool", lambda e, o=ccv[:, cs], s=self.ccsc[cs].rearrange("(c p) n -> p c n", p=128): e.dma_start(out=o, in_=s),
                  (), (cc_res,))
        ctv = []
        for cs in range(2):
            v = self.ring[:, 2 * cs:2 * cs + 2, :].rearrange("p s n -> p (s n)").rearrange("p (c n) -> p c n", c=8)
            rs = (self.ring_res[2 * cs], self.ring_res[2 * cs + 1])
            P.dma("pool", lambda e, o=v, s=self.ctst[cs].rearrange("(c p) n -> p c n", p=128): e.dma_start(out=o, in_=s),
                  (), rs)
            ctv.append((v, rs))
        obb = self.ob[:].rearrange("p c t -> p (c t)").bitcast(BF16)

        def Y(g, cs):
            i = g * 2 + cs
            return (obb[:, i * 4096:(i + 1) * 4096].rearrange("p (a n) -> p a n", a=8),
                    (self.ob_res[2 * i], self.ob_res[2 * i + 1]))

        for g in range(4):
            for cs in range(2):
                yv, yres = Y(g, cs)
                for tch in range(8):
                    b = self.proj_bank()
                    for cc in range(4):
                        self.mm(self.ps[:, b, :], self.hT[:, 4 * g + cc, tch * 128:(tch + 1) * 128], ccv[:, cs, cc, :],
                                cc == 0, cc == 3, (self.hT_res, cc_res), (self.ps_res[b],), cc == 3)
                    self.evac(yv[:, tch, :], self.ps[:, b, :], (self.ps_res[b],), yres)
        for g in range(4):
            for cq in range(4):
                for tt in range(2):
                    b = self.proj_bank()
                    n = 0
                    for cs in range(2):
                        yv, yres = Y(g, cs)
                        cv, cres = ctv[cs]
                        for tch in range(8):
                            self.mm(self.ps[:, b, :], yv[:, tch, cq * 128:(cq + 1) * 128], cv[:, tch, tt * 512:(tt + 1) * 512],
                                    n == 0, n == 15, yres + cres, (self.ps_res[b],), n == 15)
                            n += 1
                    self.evac(self.hT[:, 4 * g + cq, tt * 512:(tt + 1) * 512], self.ps[:, b, :], (self.ps_res[b],), (self.hT_res,))
        self.project_fm(self.fw, lambda n: (self.ob[:, n, :], self.ob_res[n]), done=self.stats_acc)
        P.barrier()

    def project_fm(self, W, dst, src=None, src_res=None, done=None):
        if src is None:
            src, src_res = self.hT, self.hT_res
        for nt in range(8):
            wt, wres = self.ring_load(W[:, nt * 256:(nt + 1) * 256], 16, 256)
            for j in range(2):
                n = nt * 2 + j
                o, ores = dst(n)
                for t in range(2):
                    b = self.proj_bank()
                    for kc in range(16):
                        self.mm(self.ps[:, b, :], wt[:, kc, j * 128:(j + 1) * 128], src[:, kc, t * 512:(t + 1) * 512],
                                kc == 0, kc == 15, (wres, src_res), (self.ps_res[b],), kc == 15)
                    self.evac(o[:, t * 512:(t + 1) * 512], self.ps[:, b, :], (self.ps_res[b],), (ores,))
                if done is not None:
                    done(n)

    def gla(self):
        P = self.P
        P.barrier()
        self.ring_nslot = 3
        self.ring_next = 0
        SC = 256.0 ** -0.5
        U = self.U
        qT = U[:, 0:1024].bitcast(BF16).rearrange("p (e t) -> p e t", e=2)
        kT = U[:, 1024:2048].bitcast(BF16).rearrange("p (e t) -> p e t", e=2)
        khT = U[:, 2048:3072].bitcast(BF16).rearrange("p (e t) -> p e t", e=2)
        khat = U[:, 3072:4096].bitcast(BF16).rearrange("p (c k) -> p c k", c=8)
        cs_sb = U[:, 4096:6144].rearrange("p (a t) -> p a t", a=2)
        lrT = U[:, 6144:7168]
        gn_sb = U[:, 7168:7680]
        t12 = U[:, 7680:8704].rearrange("p (a n) -> p a n", a=2)
        R2 = self.ring[:, 3:6, :].rearrange("p s n -> p (s n)").bitcast(F32)
        S_f32 = R2[:, 0:1024].rearrange("p (e n) -> p e n", e=2)
        S_exit = R2[:, 1024:2048].rearrange("p (e n) -> p e n", e=2)
        S_bf = R2[:, 2048:3072].bitcast(BF16).rearrange("p (b e n) -> p b e n", b=2, e=2)
        gc_sb = R2[:, 3072:4480]
        masks = gc_sb[:, 0:256].rearrange("p (a n) -> p a n", a=2)
        ident_f = gc_sb[:, 256:384]
        cmask = gc_sb[:, 384:1408]
        wg_sb = R2[:, 4480:4992].rearrange("p (b n) -> p b n", b=2)
        PT = R2[:, 4992:5120].bitcast(BF16).rearrange("p (b n) -> p b n", b=2)
        X = R2[:, 5120:6144].rearrange("p (e n) -> p e n", e=2)
        on_bf = X[:, 0, :].bitcast(BF16).rearrange("p (b n) -> p b n", b=2)
        flags_sb, hstat, decay, Ltot, ident_bf = self.g_flags, self.g_hstat, self.g_decay, self.g_ltot, self.g_ident
        l1 = self.xin
        L = self.tmp
        rq = self.rstd[:].rearrange("p (e n) -> p e n", e=2)
        rk = self.sq[:].rearrange("p a t -> p (a t)").bitcast(F32).rearrange("p (e n) -> p e n", e=2)
        obb = self.ob[:].rearrange("p c t -> p (c t)").bitcast(BF16)
        oacc = self.ob[:, 0:4, :].rearrange("p c t -> p (c t)").rearrange("p (c n) -> p c n", c=8)
        v_tm = obb[:, 8192:12288].rearrange("p (c n) -> p c n", c=8)
        sr_tm = obb[:, 12288:16384].rearrange("p (c n) -> p c n", c=8)
        OT = obb[:, 16384:32768].rearrange("p (c t) -> p c t", c=16)
        hTf = self.hT[:].rearrange("p c t -> p (c t)").bitcast(F32).rearrange("p (c t) -> p c t", c=8)

        R = lambda n: Res(n)
        r_q, r_k, r_kh, r_khat, r_cs, r_lr, r_gn, r_t12 = R("qT"), R("kT"), R("khT"), R("khat"), R("cs"), R("lrT"), R("gn"), [R("t12a"), R("t12b")]
        r_Sf, r_Se, r_Sb, r_gc, r_wg, r_PT, r_X = R("Sf"), R("Se"), [R("Sb0"), R("Sb1")], R("gc"), [R("wg0"), R("wg1")], [R("PT0"), R("PT1")], R("X")
        r_fl, r_hs, r_dec, r_lt, r_id = R("flags"), R("hstat"), R("decay"), R("ltot"), R("ident")
        r_l1, r_L, r_rq, r_rk = R("l1"), R("L"), R("rq"), R("rk")
        r_S = [R("S0"), R("S1"), R("S2"), R("S3")]
        r_dcp = R("decp")
        decp = self.g_decp
        r_oacc, r_v, r_sr, r_OT, r_on = R("oacc"), R("v"), R("sr"), R("OT"), [R("on0"), R("on1")]
        hres = self.hT_res

        def sp_load(out, in_, res):
            P.dma("sp", lambda e: e.dma_start(out=out, in_=in_), (), (res,))

        sp_load(cs_sb[:, 0, :], self.ropecs[0], r_cs)
        sp_load(cs_sb[:, 1, :], self.ropecs[1], r_cs)
        sp_load(gn_sb, self.gnorm, r_gn)
        sp_load(gc_sb, self.gconst, r_gc)
        sp_load(flags_sb[:], self.flags, r_fl)
        P.op("act", lambda e: e.copy(ident_bf[:], ident_f), (r_gc,), (r_id,))
        P.op("dve", lambda e: e.memset(lrT, 1.0), (), (r_lr,))
        wt, wres = self.ring_load(self.win[:, 8192:8224], 16, 32)
        for t in range(2):
            b = self.proj_bank()
            for kc in range(16):
                self.mm(self.ps[0:32, b, :], wt[:, kc, :], self.hT[:, kc, t * 512:(t + 1) * 512], kc == 0, kc == 15,
                        (wres, hres), (self.ps_res[b],), kc == 15)
            P.op("dve", lambda e, o=lrT[0:32, t * 512:(t + 1) * 512], i=self.ps[0:32, b, :]: e.tensor_copy(o, i),
                 (self.ps_res[b],), (r_lr,))

        wgi = 0
        pti = 0
        t12i = 0
        oni = 0
        for hd in range(4):
            base = hd * 2048
            def v_tiles(h_):
                c0 = h_ * 2048 + 1024
                return (self.ring_load(self.win[:, c0:c0 + 256], 16, 256), self.ring_load(self.win[:, c0 + 256:c0 + 512], 16, 256))

            def v_unit(tiles, tch):
                b = self.proj_bank()
                for half, (tl, rl) in enumerate(tiles):
                    for kc in range(16):
                        self.mm(self.ps[:, b, half * 256:(half + 1) * 256], self.hT[:, kc, tch * 128:(tch + 1) * 128], tl[:, kc, :],
                                kc == 0, kc == 15, (rl, hres), (self.ps_res[b],), kc == 15)
                self.evac(v_tm[:, tch, :], self.ps[:, b, :], (self.ps_res[b],), (r_v,))

            if hd == 0:
                vt = v_tiles(0)
                for tch in range(8):
                    v_unit(vt, tch)
            for d in range(2):
                rt, rrt = self.ring_load(self.win[:, base + 1536 + d * 256:base + 1536 + d * 256 + 256], 16, 256)
                r_units = list(range(8))

                def r_unit(tch, d=d, rt=rt, rrt=rrt):
                    b = self.proj_bank()
                    for kc in range(16):
                        self.mm(self.ps[:, b, 0:256], self.hT[:, kc, tch * 128:(tch + 1) * 128], rt[:, kc, :],
                                kc == 0, kc == 15, (rrt, hres), (self.ps_res[b],), kc == 15)
                    self.evac(sr_tm[:, tch, d * 256:(d + 1) * 256], self.ps[:, b, 0:256], (self.ps_res[b],), (r_sr,))

                wq, rwq = self.ring_load(self.win[:, base + d * 512:base + d * 512 + 256], 16, 256)
                wk, rwk = self.ring_load(self.win[:, base + d * 512 + 256:base + d * 512 + 512], 16, 256)
                bi = wgi % 2
                wgi += 1
                sp_load(wg_sb[0:33, bi, :], self.wg[:, d * 1024 + hd * 256:d * 1024 + hd * 256 + 256], r_wg[bi])
                self.proj_nbanks = 8
                for e_ in range(2):
                    for t in range(2):
                        b = self.proj_bank()
                        self.mm(self.ps[:, b, :], wg_sb[0:33, bi, e_ * 128:(e_ + 1) * 128], lrT[0:33, t * 512:(t + 1) * 512], True, True,
                                (r_wg[bi], r_lr), (self.ps_res[b],), True)
                        ti = t12i % 2
                        t12i += 1
                        self.act(t12[:, ti, :], self.ps[:, b, :], AF.Exp, (self.ps_res[b],), (r_t12[ti],), scale=-1.0)
                        self.act(l1[:, e_, t * 512:(t + 1) * 512], t12[:, ti, :], AF.Ln, (r_t12[ti], self.onef_res), (r_l1, r_S[2], r_S[3]),
                                 bias=self.onef[:, 0:1])
                for e_ in range(2):
                    P.op("dve", lambda e, o=L[:, e_, :], m=cmask, x=l1[:, e_, :]: e.tensor_tensor_scan(o, m, x, 0.0, ALU.mult, ALU.add),
                         (r_gc, r_l1), (r_L,))
                    lv = L[:, e_, :].rearrange("p (c j) -> p c j", j=128)
                    P.op("act", lambda e, o=Ltot[:, e_, :], i=lv[:, :, 127]: e.copy(o, i), (r_L,), (r_lt,))
                self.act(decay[:].rearrange("p e c -> p (e c)"), Ltot[:].rearrange("p e c -> p (e c)"), AF.Exp, (r_lt,), (r_dec,),
                         scale=-1.0 / 16.0)
                if d == 1:
                    for e_ in range(2):
                        lv = L[:, e_, :].rearrange("p (c j) -> p c j", j=128)
                        lb = Ltot[:, e_, :].unsqueeze(2).to_broadcast([128, 8, 128])
                        self.tt(lv, lb, lv, ALU.subtract, (r_lt, r_L), (r_L,))
                        self.tt(L[:, e_, :], L[:, e_, :], l1[:, e_, :], ALU.add, (r_L, r_l1), (r_L,))
                for tt_ in range(2):
                    ts = slice(tt_ * 512, (tt_ + 1) * 512)
                    for wtile, rw, dst, rdst in ((wq, rwq, rq, r_rq), (wk, rwk, rk, r_rk)):
                        bE = self.proj_bank()
                        for kc in range(16):
                            self.mm(self.ps[:, bE, :], wtile[:, kc, 0:128], self.hT[:, kc, ts], kc == 0, kc == 15,
                                    (rw, hres), (self.ps_res[bE],), kc == 15)
                        bO = self.proj_bank()
                        for kc in range(16):
                            self.mm(self.ps[:, bO, :], wtile[:, kc, 128:256], self.hT[:, kc, ts], kc == 0, kc == 15,
                                    (rw, hres), (self.ps_res[bO],), kc == 15)
                        pE, pO = self.ps[:, bE, :], self.ps[:, bO, :]
                        rE, rO = (self.ps_res[bE],), (self.ps_res[bO],)
                        cos, sin = cs_sb[:, 0, ts], cs_sb[:, 1, ts]
                        self.tt(dst[:, 0, :], pE, cos, ALU.mult, rE + (r_cs,), (rdst,))
                        self.tt(dst[:, 1, :], pE, sin, ALU.mult, rE + (r_cs,), (rdst,))
                        self.tt(t12[:, 0, :], pO, sin, ALU.mult, rO + (r_cs,), (r_t12[0],))
                        self.tt(t12[:, 1, :], pO, cos, ALU.mult, rO + (r_cs,), (r_t12[1],))
                        self.tt(dst[:, 0, :], dst[:, 0, :], t12[:, 0, :], ALU.subtract, (rdst, r_t12[0]), (rdst,))
                        self.tt(dst[:, 1, :], dst[:, 1, :], t12[:, 1, :], ALU.add, (rdst, r_t12[1]), (rdst,))
                        for _ in range(2):
                            if r_units:
                                r_unit(r_units.pop(0))
                    self.act(X, L[:, :, ts], AF.Exp, (r_L,), (r_X,), scale=-1.0 / 16.0)
                    self.stt(qT[:, :, ts], rq, SC, X, ALU.mult, ALU.mult, (r_rq, r_X), (r_q,))
                    self.act(X, L[:, :, ts], AF.Exp, (r_L,), (r_X,), scale=1.0 / 16.0)
                    self.tt(rk, rk, X, ALU.mult, (r_rk, r_X), (r_rk,))
                    P.op("act", lambda e, o=kT[:, :, ts], i=rk: e.copy(o, i), (r_rk,), (r_k,))
                    for e_ in range(2):
                        db = decay[:, e_, tt_ * 4:(tt_ + 1) * 4].unsqueeze(2).to_broadcast([128, 4, 128])
                        self.tt(khT[:, e_, ts].rearrange("p (c j) -> p c j", j=128), rk[:, e_, :].rearrange("p (c j) -> p c j", j=128),
                                db, ALU.mult, (r_rk, r_dec), (r_kh,))
                self.proj_nbanks = 4
                for e_ in range(2):
                    b = self.proj_bank()
                    psb = self.ps[:, b, :].bitcast(BF16).rearrange("p (c k) -> p c k", c=8)
                    for tch in range(8):
                        P.op("pe", lambda e, o=psb[:, tch, :], i=khT[:, e_, tch * 128:(tch + 1) * 128]: e.transpose(o, i, ident_bf[:]),
                             (r_kh, r_id), (self.ps_res[b],), tch == 7)
                    self.evac(khat[:, :, e_ * 128:(e_ + 1) * 128], psb, (self.ps_res[b],), (r_khat,))
                fcol = 8 if d == 0 else 16
                for e_ in range(2):
                    self.tt(decp[:, e_, :], decay[:, e_, :], flags_sb[:, fcol:fcol + 8], ALU.mult, (r_dec, r_fl), (r_dcp,))
                PTa = khT[:].rearrange("p e t -> p (e t)")[:, 0:1024].rearrange("p (c j) -> p c j", c=8)
                for g in range(2):
                    b = self.proj_bank()
                    for cc in range(4):
                        c = g * 4 + cc
                        cs_ = slice(c * 128, (c + 1) * 128)
                        for e_ in range(2):
                            self.mm(self.ps[:, b, cc * 128:(cc + 1) * 128], kT[:, e_, cs_], qT[:, e_, cs_], e_ == 0, e_ == 1,
                                    (r_k, r_q, r_khat), (self.ps_res[b],), e_ == 1)
                    mb = masks[:, d, :].unsqueeze(1).to_broadcast([128, 4, 128])
                    self.tt(PTa[:, g * 4:(g + 1) * 4, :], self.ps[:, b, :].rearrange("p (c j) -> p c j", c=4), mb, ALU.mult,
                            (self.ps_res[b], r_gc), (r_kh,))
                sbufs = [S_f32, S_exit, l1[:, 0, :].rearrange("p (e n) -> p e n", e=2), l1[:, 1, :].rearrange("p (e n) -> p e n", e=2)]
                sres = [r_S[0], r_S[1], r_S[2], r_S[3]]
                sext = [(), (), (r_l1,), (r_l1,)]
                P.dma("sp", lambda e, o=sbufs[3], i=self.sinit[d, hd].rearrange("e p n -> p e n"): e.dma_start(out=o, in_=i),
                      (), (sres[3], r_l1))
                P.op("dve", lambda e, o=S_bf[:, 0], i=sbufs[3]: e.tensor_copy(o, i), (sres[3],), (r_Sb[0],))

                def sn(s):
                    c = s if d == 0 else 7 - s
                    for e_ in range(2):
                        bk = 4 + 2 * (s % 2) + e_
                        self.mm(self.ps[:, bk, :], khat[:, c, e_ * 128:(e_ + 1) * 128], v_tm[:, c, :], True, True,
                                (r_khat, r_v), (self.ps_res[bk],), True)

                sn(0)
                sn(1)
                pending_add = None
                for s in range(8):
                    c = s if d == 0 else 7 - s
                    cs_ = slice(c * 128, (c + 1) * 128)
                    cur, prv = s % 4, (s + 3) % 4
                    sb = s % 2
                    for e_ in range(2):
                        bk = 4 + 2 * (s % 2) + e_
                        self.stt(sbufs[cur][:, e_, :], sbufs[prv][:, e_, :], decp[:, e_, c:c + 1], self.ps[:, bk, :], ALU.mult, ALU.add,
                                 (sres[prv], r_dcp, self.ps_res[bk]), (sres[cur],) + sext[cur])
                    if pending_add is not None:
                        pending_add()
                        pending_add = None
                    bo = self.proj_bank()
                    self.mm(self.ps[:, bo, :], PTa[:, c, :], v_tm[:, c, :], True, False, (r_kh, r_v), (self.ps_res[bo],), False)
                    for e_ in range(2):
                        self.mm(self.ps[:, bo, :], qT[:, e_, cs_], S_bf[:, sb, e_, :], False, e_ == 1, (r_q, r_Sb[sb]), (self.ps_res[bo],), e_ == 1)
                    if s + 2 < 8:
                        sn(s + 2)
                    if s < 7:
                        self.act(S_bf[:, 1 - sb].rearrange("p e n -> p (e n)"), sbufs[cur].rearrange("p e n -> p (e n)"), AF.Copy,
                                 (sres[cur], r_fl), (r_Sb[1 - sb],), scale=flags_sb[:, s:s + 1])
                    if d == 0:
                        P.op("act", lambda e, o=oacc[:, c, :], i=self.ps[:, bo, :]: e.copy(o, i), (self.ps_res[bo],), (r_oacc,))
                    else:
                        def _add(o=oacc[:, c, :], i=self.ps[:, bo, :], r=self.ps_res[bo]):
                            self.tt(o, i, o, ALU.add, (r, r_oacc), (r_oacc,))
                        pending_add = _add
                    if s % 2 == 1:
                        q_ = s // 2 if d == 0 else (7 - s) // 2
                        P.dma("sp", lambda e, o=self.st_out[d, hd, q_].rearrange("e p n -> p e n"), i=sbufs[cur]: e.dma_start(out=o, in_=i),
                              (sres[cur],), (), is_output=True)
                if pending_add is not None:
                    pending_add()
            srf = obb[:, 12288:16384]
            self.act(srf, srf, AF.Silu, (r_sr,), (r_sr,))
            for tch in range(8):
                ti = t12i % 2
                t12i += 1
                self.act(t12[:, ti, :], oacc[:, tch, :], AF.Square, (r_oacc,), (r_t12[ti],))
                P.op("dve", lambda e, o=hstat[:, tch:tch + 1], i=t12[:, ti, :]: e.reduce_sum(o, i, mybir.AxisListType.X),
                     (r_t12[ti],), (r_hs,))
            self.act(hstat[:, 8:16], hstat[:, 0:8], AF.Sqrt, (r_hs, self.onef_res), (r_hs,), bias=self.epsb[:, 0:1], scale=1.0 / 512.0)
            P.op("dve", lambda e, o=hstat[:, 8:16], i=hstat[:, 8:16]: e.reciprocal(o, i), (r_hs,), (r_hs,))
            vt = v_tiles(hd + 1) if hd < 3 else None
            for tch in range(8):
                if vt is not None:
                    v_unit(vt, tch)
                ti = t12i % 2
                t12i += 1
                self.stt(t12[:, ti, :], oacc[:, tch, :], hstat[:, 8 + tch:9 + tch], gn_sb, ALU.mult, ALU.mult,
                         (r_oacc, r_hs, r_gn), (r_t12[ti],))
                oi = oni % 2
                oni += 1
                self.tt(on_bf[:, oi, :], t12[:, ti, :], sr_tm[:, tch, :], ALU.mult, (r_t12[ti], r_sr), (r_on[oi], r_X))
                b = self.proj_bank()
                psb = self.ps[:, b, :].bitcast(BF16)[:, 0:512].rearrange("p (c k) -> p c k", c=4)
                for dvc in range(4):
                    P.op("pe", lambda e, o=psb[:, dvc, :], i=on_bf[:, oi, dvc * 128:(dvc + 1) * 128]: e.transpose(o, i, ident_bf[:]),
                         (r_on[oi], r_X, r_id), (self.ps_res[b],), dvc == 3)
                self.evac(OT[:, hd * 4:(hd + 1) * 4, tch * 128:(tch + 1) * 128], psb, (self.ps_res[b],), (r_OT,))
        P.barrier()
        self.project_fm(self.wout, lambda n: (self.ob[:, n, :], self.ob_res[n]) if n < 8 else (hTf[:, n - 8, :], self.hT_res),
                        src=OT, src_res=r_OT)
        P.barrier()
        for i in range(8):
            self.stats_acc(i)
        for i in range(8):
            self.evac(self.ob[:, 8 + i, :], hTf[:, i, :], (self.hT_res,), (self.ob_res[8 + i],))
            self.stats_acc(8 + i)
        P.barrier()
        self.ring_nslot = NSLOT

    def build(self):
        stages = ["ffn00", "mix0", "ffn01", "ffn10", "mix1", "ffn11"]
        stop = self.stop_after or "ffn11"
        nst = stages.index(stop) + 1
        self.setup()
        x_src, x_res = self.xT, None
        for si in range(nst):
            l, k = si // 3, si % 3
            last = si == nst - 1
            if si == 0:
                self.load_x0()
            self.coef_AB(l, k)
            self.prenorm(l, k)
            if k == 0:
                self.ffn(l, 0)
            elif k == 2:
                self.ffn(l, 1)
            elif l == 0:
                self.fourier()
            else:
                self.gla()
            self.coef_C(l, k)
            if last:
                self.residual(l, k, x_src, x_res, self.yT, None, is_output=True)
            else:
                self.residual(l, k, x_src, x_res, self.xd, self.xd_res)
            x_src, x_res = self.xd, self.xd_res
        self.finish()
        self.P.check_deadlock()
        self.P.emit()
        return self.nc

    def finish(self):
        pass


def _pcol(v):
    v = np.asarray(v, np.float32)
    return np.ascontiguousarray(v.reshape(-1, 128).T)


def _wg_aug(inp):
    kp = _kperm()
    wg = np.zeros((33, 2048), np.float32)
    for d in range(2):
        wg[16 * d:16 * d + 16, d * 1024:(d + 1) * 1024] = inp["gla_w_gate_up"][0, d][:, kp]
        wg[32, d * 1024:(d + 1) * 1024] = inp["gla_b_gate"][0, d][kp]
    return wg


def _core_inputs(core, inp, common):
    if core < 4:
        x = inp["x_prompt"][core * 4:(core + 1) * 4].reshape(T, D)
        cond = inp["c_ctx"]
    else:
        x = inp["x_sample"][core - 4]
        cond = inp["c"][core - 4]
    m = dict(common)
    ccsc, ctst = _dft_consts(256 if core < 4 else 1024)
    m["ccsc"] = ccsc
    m["ctst"] = ctst
    m["ropecs"] = _rope_tables(core >= 4)
    if core >= 4:
        s0 = inp["state_gla"][core - 4, 0]
        m["sinit"] = np.ascontiguousarray(s0.reshape(2, 4, 128, 2, 512).transpose(0, 1, 3, 2, 4), dtype=np.float32)
        fl = np.ones(8, np.float32)
    else:
        m["sinit"] = np.zeros((2, 4, 2, 128, 512), np.float32)
        fl = np.zeros(8, np.float32)
        fl[0::2] = 1.0
    fprev = np.concatenate([[1.0], fl[:7]]).astype(np.float32)
    m["flags"] = np.ascontiguousarray(np.broadcast_to(np.concatenate([fl, fprev, fprev[::-1]])[None, :], (128, 24)), dtype=np.float32)
    m["xT"] = np.ascontiguousarray(x.T)
    m["cond"] = _pcol(cond)
    return m


def _dft_consts(seq_len):
    n = np.arange(512)
    ang = 2.0 * np.pi * ((n[:, None] * n[None, :]) % 512) / 512.0
    ccsc = np.stack([np.cos(ang), np.sin(ang)]) / np.sqrt(512.0)
    t = np.arange(seq_len)
    a2 = 2.0 * np.pi * ((t[:, None] * t[None, :]) % seq_len) / seq_len
    c1, s1 = np.cos(a2) / np.sqrt(seq_len), -np.sin(a2) / np.sqrt(seq_len)
    ct = np.zeros((T, T)); st = np.zeros((T, T))
    for i in range(T // seq_len):
        sl = slice(i * seq_len, (i + 1) * seq_len)
        ct[sl, sl] = c1
        st[sl, sl] = s1
    return ccsc.astype(np.float32), np.stack([ct, st]).astype(np.float32)


def _gla_perm():
    ev = np.arange(0, 256, 2)
    od = np.arange(1, 256, 2)
    cols = []
    for hd in range(4):
        for blk in range(4):
            b0 = blk * 1024 + hd * 256
            cols.append(b0 + ev)
            cols.append(b0 + od)
        cols.append(4096 + hd * 512 + np.arange(512))
        cols.append(6144 + hd * 512 + np.arange(512))
    cols.append(np.arange(8192, 8224))
    return np.concatenate(cols)


def _kperm():
    ev = np.arange(0, 256, 2)
    od = np.arange(1, 256, 2)
    return np.concatenate([hd * 256 + np.concatenate([ev, od]) for hd in range(4)])


def _gla_consts():
    j = np.arange(128)
    mf = (j[:, None] <= j[None, :]).astype(np.float32)
    mb = (j[:, None] >= j[None, :]).astype(np.float32)
    ident = np.eye(128, dtype=np.float32)
    cm = np.ones((128, T), np.float32)
    cm[:, ::128] = 0.0
    return np.ascontiguousarray(np.concatenate([mf, mb, ident, cm], axis=1))


def _rope_tables(is_sample):
    if not is_sample:
        return np.stack([np.ones((128, T), np.float32), np.zeros((128, T), np.float32)])
    t = np.arange(T)
    row = (t // 64).astype(np.float32)
    col = (t % 64).astype(np.float32)
    inv = (np.float32(10000.0) ** (-np.arange(64, dtype=np.float32) / np.float32(64.0))).astype(np.float32)
    ang = np.concatenate([row[:, None] * inv[None, :], col[:, None] * inv[None, :]], axis=1)
    return np.stack([np.cos(ang).T, np.sin(ang).T]).astype(np.float32)


def _common_inputs(inp):
    nv = np.zeros((128, 193), np.float32)
    nv[0, 192] = nv[32, 192] = nv[64, 192] = 1.0
    for l in range(2):
        for k in range(3):
            nv[:, ((l * 3 + k) * 2 + 0) * 16:((l * 3 + k) * 2 + 0) * 16 + 16] = _pcol(inp["norm_pre"][l, k])
            nv[:, ((l * 3 + k) * 2 + 1) * 16:((l * 3 + k) * 2 + 1) * 16 + 16] = _pcol(inp["norm_post"][l, k])
    ab = np.concatenate([_pcol(inp["ada_b"][0]), _pcol(inp["ada_b"][1])], axis=1)
    return {
        "ada_w": np.ascontiguousarray(inp["ada_w"], dtype=np.float32),
        "ada_bT": np.ascontiguousarray(ab),
        "nvec": nv,
        "ffn_gu": np.ascontiguousarray(inp["ffn_w_gate_up"], dtype=np.float32),
        "ffn_dn": np.ascontiguousarray(inp["ffn_w_down"], dtype=np.float32),
        "fw": np.ascontiguousarray(inp["fourier_w"][0], dtype=np.float32),
        "win": np.ascontiguousarray(inp["gla_w_in"][0][:, _gla_perm()], dtype=np.float32),
        "wg": _wg_aug(inp),
        "gnorm": np.ascontiguousarray(np.broadcast_to(inp["gla_norm"][0][None, :], (128, 512)), dtype=np.float32),
        "wout": np.ascontiguousarray(inp["gla_w_out"][0], dtype=np.float32),
        "gconst": _gla_consts(),
    }


def _build_cached(stop_after=None):
    b = Builder(stop_after=stop_after)
    return b.build()


def run_partial(inp, stop_after, cores=(0, 4), trace=False):
    nc = _build_cached(stop_after)
    common = _common_inputs(inp)
    in_maps = [_core_inputs(c, inp, common) for c in cores]
    names = set()
    for alloc in nc.allocations:
        try:
            if alloc.kind == "ExternalInput":
                names.add(alloc.memorylocations[0].name)
        except Exception:
            pass
    if names:
        in_maps = [{k: v for k, v in m.items() if k in names} for m in in_maps]
    res = run_bass_kernel_spmd(nc, in_maps, core_ids=list(range(len(cores))), trace=trace)
    return res


def kernel(**inputs):
    inp = {k: np.asarray(v) for k, v in inputs.items()}
    res = run_partial(inp, None, cores=tuple(range(NCORES)))
    y_prompt = np.empty((16, 256, D), np.float32)
    y_sample = np.empty((2, T, D), np.float32)
    new_state = np.empty((16, 1, 2, 4, 256, 512), np.float32)
    for c in range(NCORES):
        r = res.results[c]
        y = np.asarray(r["yT"]).T
        if c < 4:
            y_prompt[c * 4:(c + 1) * 4] = y.reshape(4, 256, D)
            st = np.asarray(r["st_out"])
            st = st.transpose(2, 0, 1, 4, 3, 5).reshape(4, 2, 4, 256, 512)
            new_state[c * 4:(c + 1) * 4, 0] = st
        else:
            y_sample[c - 4] = y
    return (y_prompt, y_sample, new_state)
```

```python
import numpy as np
import concourse.bass as bass
import concourse.mybir as mybir
from concourse.bass_utils import run_bass_kernel_spmd

F32 = mybir.dt.float32
BF16 = mybir.dt.bfloat16
AF = mybir.ActivationFunctionType
ALU = mybir.AluOpType

D = 2048
T = 1024
NC_ = 16
DFF = 5632
EPS = 1e-6
NCORES = 6
NSLOT = 6
SLOT = 4096


class Res:
    __slots__ = ("w", "r", "name")

    def __init__(self, name=""):
        self.w = {}
        self.r = {}
        self.name = name


class Prog:
    COMPUTE = ("pe", "act", "dve", "pool")

    def __init__(self, nc):
        self.nc = nc
        self.streams = {e: [] for e in ("pe", "act", "dve", "pool", "sp")}
        self.sem = {}
        self.count = {e: 0 for e in self.COMPUTE}
        self.known = {e: {} for e in self.streams}
        for e in self.COMPUTE:
            self.sem[e] = nc.alloc_semaphore(name="prog_" + e)
        self.dsem = {"sp": [nc.alloc_semaphore(name=f"dsp{i}") for i in range(8)],
                     "pool": [nc.alloc_semaphore(name=f"dpl{i}") for i in range(8)]}
        self.dcnt = {q: [0] * len(v) for q, v in self.dsem.items()}
        self.drr = {q: 0 for q in self.dsem}
        self.semobj = {}
        for e in self.COMPUTE:
            self.semobj[id(self.sem[e])] = self.sem[e]
        for q in self.dsem:
            for s in self.dsem[q]:
                self.semobj[id(s)] = s
        self.out_tokens = []

    def _collect(self, eng, reads, writes, own):
        need = {}
        war = {}
        for r in reads:
            for k, v in r.w.items():
                if need.get(k, 0) < v:
                    need[k] = v
        for w in writes:
            for k, v in w.w.items():
                if need.get(k, 0) < v:
                    need[k] = v
            for k, v in w.r.items():
                if war.get(k, 0) < v:
                    war[k] = v
        final = []
        kn = self.known[eng]
        for k, v in need.items():
            if own is not None and k == own:
                if eng == "pe":
                    continue
                if v <= self.count[eng] - 2:
                    continue
            if kn.get(k, 0) >= v:
                continue
            kn[k] = v
            final.append((k, v))
        for k, v in war.items():
            if own is not None and k == own:
                if eng == "pe" or v <= self.count[eng] - 2:
                    continue
            if kn.get(k, 0) >= v:
                continue
            kn[k] = v
            final.append((k, v))
        return final

    def _mark(self, reads, writes, key, val):
        for r in reads:
            if r.r.get(key, 0) < val:
                r.r[key] = val
        for w in writes:
            w.r.clear()
            if w.w.get(key, 0) < val:
                w.w[key] = val

    def op(self, eng, fn, reads=(), writes=(), inc=True):
        own = id(self.sem[eng])
        final = self._collect(eng, reads, writes, own)
        val = self.count[eng] + 1
        if inc:
            self.count[eng] = val
        self._mark(reads, writes, own, val)
        self.streams[eng].append((final, fn, ("c", own) if inc else None))

    def dma(self, q, fn, reads=(), writes=(), is_output=False):
        i = self.drr[q]
        self.drr[q] = (i + 1) % len(self.dsem[q])
        s = self.dsem[q][i]
        key = id(s)
        own = id(self.sem[q]) if q in self.sem else None
        final = self._collect(q, reads, writes, None)
        prev = self.dcnt[q][i]
        if prev > 0 and self.known[q].get(key, 0) < 16 * prev:
            self.known[q][key] = 16 * prev
            final.append((key, 16 * prev))
        self.dcnt[q][i] = prev + 1
        val = 16 * (prev + 1)
        self._mark(reads, writes, key, val)
        self.streams[q].append((final, fn, ("d", key)))
        if is_output:
            self.out_tokens.append((key, val))

    def barrier(self):
        targets = {}
        for e in self.COMPUTE:
            if self.count[e] > 0:
                targets[id(self.sem[e])] = self.count[e]
        for q in self.dsem:
            for i, s in enumerate(self.dsem[q]):
                if self.dcnt[q][i] > 0:
                    targets[id(s)] = 16 * self.dcnt[q][i]
        for e in self.streams:
            own = id(self.sem[e]) if e in self.sem else None
            waits = []
            for k, v in targets.items():
                if k == own:
                    continue
                if self.known[e].get(k, 0) >= v:
                    continue
                self.known[e][k] = v
                waits.append((k, v))
            if waits:
                self.streams[e].append((waits, None, None))

    def check_deadlock(self):
        sems = {}
        pos = {e: 0 for e in self.streams}
        n = {e: len(s) for e, s in self.streams.items()}
        progress = True
        while progress:
            progress = False
            for e, st in self.streams.items():
                while pos[e] < n[e]:
                    waits, fn, inc = st[pos[e]]
                    if all(sems.get(k, 0) >= v for k, v in waits):
                        if inc is not None:
                            sems[inc[1]] = sems.get(inc[1], 0) + (1 if inc[0] == "c" else 16)
                        pos[e] += 1
                        progress = True
                    else:
                        break
        stuck = {e: (pos[e], n[e]) for e in pos if pos[e] < n[e]}
        if stuck:
            msg = []
            for e, (p, _) in stuck.items():
                waits = self.streams[e][p][0]
                msg.append((e, p, [(self.semobj[k].name if hasattr(self.semobj[k], "name") else k, v, sems.get(k, 0)) for k, v in waits]))
            raise RuntimeError(f"DEADLOCK: {stuck} {msg}")
        return {e: n[e] for e in n}

    def emit(self):
        nc = self.nc
        fin = {}
        for k, v in self.out_tokens:
            fin[k] = max(fin.get(k, 0), v)
        semobj = self.semobj
        streams = self.streams

        def run(e, name):
            for waits, fn, inc in streams[name]:
                for k, v in waits:
                    e.wait_ge(semobj[k], v)
                if fn is None:
                    continue
                ins = fn(e)
                if inc is not None:
                    if inc[0] == "c":
                        ins.then_inc(semobj[inc[1]], 1)
                    else:
                        ins.then_inc(semobj[inc[1]], 16)
            if name == "sp":
                for k, v in fin.items():
                    e.wait_ge(semobj[k], v)

        with nc.Block() as block:
            @block.tensor
            def _(e):
                run(e, "pe")

            @block.scalar
            def _(e):
                run(e, "act")

            @block.vector
            def _(e):
                run(e, "dve")

            @block.gpsimd
            def _(e):
                run(e, "pool")

            @block.sync
            def _(e):
                run(e, "sp")


class Builder:
    def __init__(self, stop_after=None):
        self.stop_after = stop_after
        nc = self.nc = bass.Bass("TRN2", target_bir_lowering=False)
        self.P = Prog(nc)
        P = self.P
        dt = nc.dram_tensor
        self.xT = dt("xT", [D, T], F32, kind="ExternalInput").ap()
        self.cond = dt("cond", [128, 16], F32, kind="ExternalInput").ap()
        self.ada_w = dt("ada_w", [2, D, 9 * D], F32, kind="ExternalInput").ap()
        self.ada_bT = dt("ada_bT", [128, 2 * 144], F32, kind="ExternalInput").ap()
        self.nvec = dt("nvec", [128, 193], F32, kind="ExternalInput").ap()
        self.ffn_gu = dt("ffn_gu", [2, 2, D, 2 * DFF], F32, kind="ExternalInput").ap()
        self.ffn_dn = dt("ffn_dn", [2, 2, DFF, D], F32, kind="ExternalInput").ap()
        self.ccsc = dt("ccsc", [2, 512, 512], F32, kind="ExternalInput").ap()
        self.ctst = dt("ctst", [2, T, T], F32, kind="ExternalInput").ap()
        self.fw = dt("fw", [D, D], F32, kind="ExternalInput").ap()
        self.win = dt("win", [D, 8224], F32, kind="ExternalInput").ap()
        self.wg = dt("wg", [33, 2048], F32, kind="ExternalInput").ap()
        self.gnorm = dt("gnorm", [128, 512], F32, kind="ExternalInput").ap()
        self.wout = dt("wout", [D, D], F32, kind="ExternalInput").ap()
        self.ropecs = dt("ropecs", [2, 128, T], F32, kind="ExternalInput").ap()
        self.sinit = dt("sinit", [2, 4, 2, 128, 512], F32, kind="ExternalInput").ap()
        self.flags = dt("flags", [128, 24], F32, kind="ExternalInput").ap()
        self.gconst = dt("gconst", [128, 1408], F32, kind="ExternalInput").ap()
        self.st_out = dt("st_out", [2, 4, 4, 2, 128, 512], F32, kind="ExternalOutput").ap()
        self.yT = dt("yT", [D, T], F32, kind="ExternalOutput").ap()
        self.xd = dt("xd", [D, T], F32, kind="Internal").ap()
        self.xd_res = [Res(f"xd{c}") for c in range(NC_)]

        A = nc.alloc_sbuf_tensor
        self.hT = A("hT", [128, NC_, T], BF16)
        self.hT_res = Res("hT")
        self.ob = A("ob", [128, NC_, T], F32)
        self.ob_res = [Res(f"ob{c}") for c in range(NC_)]
        self.ring = A("ring", [128, NSLOT, SLOT], BF16)
        self.ring_res = [Res(f"ring{i}") for i in range(NSLOT)]
        self.ring_next = 0
        self.ring_nslot = NSLOT
        self.U = A("U", [128, 8704], F32)
        self.hid = self.U[:, 0:4096].bitcast(BF16).rearrange("p (a b t) -> p a b t", a=2, b=4)
        self.hid_res = [Res("hid0"), Res("hid1")]
        self.sg = self.U[:, 4096:4864].bitcast(BF16).rearrange("p (a n) -> p a n", a=3)
        self.sg_res = [Res(f"sg{i}") for i in range(3)]
        self.sg_next = 0
        self.xin = A("xin", [128, 2, T], F32)
        self.xin_res = [Res("xin0"), Res("xin1")]
        self.sq = A("sq", [128, 2, T], BF16)
        self.sq_res = [Res("sq0"), Res("sq1")]
        self.sq_next = 0
        self.sqb = [self.sq[:, 0, :], self.sq[:, 1, :], self.U[:, 6912:7424].bitcast(BF16)]
        self.sqb_res = [self.sq_res[0], self.sq_res[1], Res("sq2")]
        self.stats_pending = []
        self.xb = [self.xin[:, 0, :], self.xin[:, 1, :], self.U[:, 4864:5888], self.U[:, 5888:6912]]
        self.xb_res = [self.xin_res[0], self.xin_res[1], Res("xb2"), Res("xb3")]
        self.rstd = A("rstd", [128, T], F32)
        self.rstd_res = Res("rstd")
        self.tmp = A("tmp", [128, 2, T], F32)
        self.tmp_res = [Res("tmp0"), Res("tmp1")]
        self.tmp_next = 0
        self.ones = A("ones", [128, 128], BF16)
        self.ones_res = Res("ones")
        self.onef = A("onef", [128, 2], F32)
        self.onef_res = Res("onef")
        self.epsb = A("epsb", [128, 1], F32)
        self.cond_sb = A("cond_sb", [128, 16], F32)
        self.cond_res = Res("cond")
        self.s_bf = A("s_bf", [128, 16], BF16)
        self.s_res = Res("s_bf")
        self.adab_sb = A("adab_sb", [128, 288], F32)
        self.adab_res = Res("adab")
        self.nvec_sb = A("nvec_sb", [128, 193], F32)
        self.nvec_res = Res("nvec")
        self.modrow = self.U[0:65, 7680:8704].rearrange("p (a n) -> p a n", a=2)
        self.modrow_res = [Res("modrow0"), Res("modrow1")]
        self.modT = A("modT", [128, 2, 144], F32)
        self.modT_res = [[Res(f"modT{l}_{m}") for m in range(9)] for l in range(2)]
        def spread(total, n=11):
            return [(total * (g + 1)) // n - (total * g) // n for g in range(n)]
        self.mod_rate = {(0, 0): spread(24), (0, 1): spread(24), (1, 0): spread(16)}
        self.modrow_next = 0
        self.mod_queue = [(l, blk) for l in range(2) for blk in range(36)]
        self.coef = A("coef", [128, 2, 9, 16], F32)
        self.coef_res = [Res("coef0"), Res("coef1")]
        self.g_flags = A("g_flags", [128, 24], F32)
        self.g_decp = A("g_decp", [128, 2, 8], F32)
        self.g_hstat = A("g_hstat", [128, 16], F32)
        self.g_decay = A("g_decay", [128, 2, 8], F32)
        self.g_ltot = A("g_ltot", [128, 2, 8], F32)
        self.g_ident = A("g_ident", [128, 128], BF16)
        self.ps = nc.alloc_psum_tensor("ps", [128, 8, 512], F32)
        self.ps_res = [Res(f"ps{i}") for i in range(8)]
        self.proj_next = 0
        self.down_next = 0

    def ring_load(self, src_ap, kc, n):
        P = self.P
        i = self.ring_next % self.ring_nslot
        self.ring_next += 1
        assert kc * n <= SLOT
        view = self.ring[:, i, 0:kc * n].rearrange("p (c n) -> p c n", c=kc)
        res = self.ring_res[i]
        src = src_ap.rearrange("(c p) n -> p c n", p=128)
        P.dma("pool", lambda e, o=view, s=src: e.dma_start(out=o, in_=s), reads=(), writes=(res,))
        return view, res

    def proj_bank(self):
        b = self.proj_next % getattr(self, "proj_nbanks", 4)
        self.proj_next += 1
        return b

    def down_bank(self):
        banks = getattr(self, "down_banks", (4, 5))
        b = banks[self.down_next % len(banks)]
        self.down_next += 1
        return b

    def mm(self, out, lhsT, rhs, start, stop, reads, writes, inc):
        self.P.op("pe", lambda e: e.matmul(out, lhsT, rhs, start=start, stop=stop), reads, writes, inc)

    def act(self, out, in_, func, reads, writes, bias=None, scale=None):
        kw = {}
        if bias is not None:
            kw["bias"] = bias
        if scale is not None:
            kw["scale"] = scale
        self.P.op("act", lambda e: e.activation(out, in_, func, **kw), reads, writes)

    def tt(self, out, in0, in1, op, reads, writes, eng="dve"):
        self.P.op(eng, lambda e: e.tensor_tensor(out, in0, in1, op), reads, writes)

    def stt(self, out, in0, scalar, in1, op0, op1, reads, writes):
        self.P.op("dve", lambda e: e.scalar_tensor_tensor(out, in0, scalar, in1, op0, op1), reads, writes)

    def setup(self):
        P = self.P
        P.op("dve", lambda e: e.memset(self.ps[:].rearrange("p b n -> p (b n)"), 0.0), (), tuple(self.ps_res))
        P.op("dve", lambda e: e.memset(self.ones[:], 1.0), (), (self.ones_res,))
        P.op("dve", lambda e: e.memset(self.onef[:], 1.0), (), (self.onef_res,))
        P.op("dve", lambda e: e.memset(self.epsb[:], EPS), (), (self.onef_res,))
        P.dma("sp", lambda e: e.dma_start(out=self.cond_sb[:], in_=self.cond), (), (self.cond_res,))
        P.dma("sp", lambda e: e.dma_start(out=self.adab_sb[:], in_=self.ada_bT), (), (self.adab_res,))
        P.dma("sp", lambda e: e.dma_start(out=self.nvec_sb[:], in_=self.nvec), (), (self.nvec_res,))
        self.act(self.s_bf[:], self.cond_sb[:], AF.Silu, (self.cond_res,), (self.s_res,))

    def mod_block(self, l, blk):
        W = self.ada_w[l]
        n0 = blk * 512
        rb = self.proj_bank()
        tiles = [self.ring_load(W[half * 1024:(half + 1) * 1024, n0:n0 + 512], 8, 512) for half in range(2)]
        groups = [[kc for kc in range(16) if kc % 3 == j] for j in range(3)]
        seq = [(j, r) for r in range(6) for j in range(3) if r < len(groups[j])]
        for idx, (j, r) in enumerate(seq):
            kc = groups[j][r]
            wt, wres = tiles[kc // 8]
            self.P.op("pe", lambda e, o=self.ps[32 * j:32 * j + 1, rb, :], a=self.s_bf[:, kc:kc + 1], b=wt[:, kc % 8, :],
                      st=(r == 0), sp=(r == len(groups[j]) - 1), tp=(0, 32 * j): e.matmul(o, a, b, start=st, stop=sp, tile_position=tp),
                      (wres, self.s_res), (self.ps_res[rb],), idx == len(seq) - 1)
        mr = self.modrow_next % 2
        self.modrow_next += 1
        self.P.op("dve", lambda e, o=self.modrow[0:65, mr, :], i=self.ps[0:65, rb, :]: e.tensor_copy(o, i),
                  (self.ps_res[rb],), (self.modrow_res[mr],))
        cb = self.proj_bank()
        for j in range(4):
            self.mm(self.ps[:, cb, j:j + 1], self.modrow[0:65, mr, j * 128:(j + 1) * 128], self.nvec_sb[0:65, 192:193],
                    True, True, (self.modrow_res[mr], self.nvec_res), (self.ps_res[cb],), j == 3)
        m = blk // 4
        self.tt(self.modT[:, l, blk * 4:blk * 4 + 4], self.ps[:, cb, 0:4], self.adab_sb[:, l * 144 + blk * 4:l * 144 + blk * 4 + 4], ALU.add,
                (self.ps_res[cb], self.adab_res), (self.modT_res[l][m],))

    def mod_pump(self, n):
        while n > 0 and self.mod_queue:
            l, blk = self.mod_queue.pop(0)
            self.mod_block(l, blk)
            n -= 1

    def mod_require(self, l, m):
        while any(ql == l and qb // 4 == m for ql, qb in self.mod_queue):
            self.mod_pump(1)

    def coef_AB(self, l, k):
        self.mod_require(l, 3 * k)
        self.mod_require(l, 3 * k + 1)
        shift = self.modT[:, l, (3 * k) * 16:(3 * k) * 16 + 16]
        scale = self.modT[:, l, (3 * k + 1) * 16:(3 * k + 1) * 16 + 16]
        gpre = self.nvec_sb[:, ((l * 3 + k) * 2 + 0) * 16:((l * 3 + k) * 2 + 0) * 16 + 16]
        cA = self.coef[:, l, 3 * k + 0, :]
        cB = self.coef[:, l, 3 * k + 1, :]
        self.stt(cA, scale, 1.0, gpre, ALU.add, ALU.mult, (self.modT_res[l][3 * k + 1], self.nvec_res), (self.coef_res[l],))
        self.P.op("dve", lambda e, o=cB, i=shift: e.tensor_copy(o, i), (self.modT_res[l][3 * k],), (self.coef_res[l],))

    def coef_C(self, l, k):
        self.mod_require(l, 3 * k + 2)
        gate = self.modT[:, l, (3 * k + 2) * 16:(3 * k + 2) * 16 + 16]
        gpost = self.nvec_sb[:, ((l * 3 + k) * 2 + 1) * 16:((l * 3 + k) * 2 + 1) * 16 + 16]
        w = 1.0 if k == 1 else 0.5
        cC = self.coef[:, l, 3 * k + 2, :]
        self.stt(cC, gate, w, gpost, ALU.mult, ALU.mult, (self.modT_res[l][3 * k + 2], self.nvec_res), (self.coef_res[l],))

    def stats_acc(self, c, defer=2):
        si = self.sq_next % 3
        self.sq_next += 1
        self.act(self.sqb[si], self.ob[:, c, :], AF.Square, (self.ob_res[c],), (self.sqb_res[si],))
        self.stats_pending.append((c, si))
        while len(self.stats_pending) > defer:
            self._stats_mm(*self.stats_pending.pop(0))

    def _stats_mm(self, c, si):
        banks = (6, 7)
        for t in range(2):
            self.mm(self.ps[:, banks[t], :], self.ones[:], self.sqb[si][:, t * 512:(t + 1) * 512],
                    c == 0, c == NC_ - 1, (self.sqb_res[si], self.ones_res), (self.ps_res[banks[t]],), t == 1)

    def stats_finish(self):
        while self.stats_pending:
            self._stats_mm(*self.stats_pending.pop(0))
        banks = (6, 7)
        tis = []
        for t in range(2):
            ti = self.tmp_next % 2
            self.tmp_next += 1
            tis.append(ti)
            self.act(self.tmp[:, ti, 0:512], self.ps[:, banks[t], :], AF.Ln, (self.ps_res[banks[t]], self.onef_res),
                     (self.tmp_res[ti],), bias=self.epsb[:, 0:1], scale=1.0 / D)
        for t in range(2):
            ti = tis[t]
            self.act(self.rstd[:, t * 512:(t + 1) * 512], self.tmp[:, ti, 0:512], AF.Exp, (self.tmp_res[ti],), (self.rstd_res,),
                     scale=-0.5)

    def load_x0(self):
        P = self.P
        for c in range(NC_):
            P.dma("sp", lambda e, o=self.ob[:, c, :], i=self.xT[c * 128:(c + 1) * 128, :]: e.dma_start(out=o, in_=i),
                  (), (self.ob_res[c],))
            self.stats_acc(c)

    def residual(self, l, k, x_src, x_src_res, x_dst, x_dst_res, is_output=False):
        P = self.P
        NB = 4

        def load(c):
            bi = c % NB
            P.dma("sp", lambda e, o=self.xb[bi], i=x_src[c * 128:(c + 1) * 128, :]: e.dma_start(out=o, in_=i),
                  (x_src_res[c],) if x_src_res else (), (self.xb_res[bi],))

        def store(c):
            P.dma("sp", lambda e, o=x_dst[c * 128:(c + 1) * 128, :], i=self.ob[:, c, :]: e.dma_start(out=o, in_=i),
                  (self.ob_res[c],), (x_dst_res[c],) if x_dst_res else (), is_output=is_output)

        for c in range(NB):
            load(c)
        self.stats_finish()
        cC = self.coef[:, l, 3 * k + 2, :]

        def scale(c):
            self.stt(self.ob[:, c, :], self.ob[:, c, :], cC[:, c:c + 1], self.rstd[:], ALU.mult, ALU.mult,
                     (self.ob_res[c], self.rstd_res, self.coef_res[l]), (self.ob_res[c],))

        scale(0)
        for c in range(NC_):
            if c + 1 < NC_:
                scale(c + 1)
            bi = c % NB
            self.tt(self.ob[:, c, :], self.ob[:, c, :], self.xb[bi], ALU.add, (self.ob_res[c], self.xb_res[bi]), (self.ob_res[c],))
            store(c)
            if c + NB < NC_:
                load(c + NB)
            if not is_output:
                self.stats_acc(c)

    def prenorm(self, l, k):
        self.stats_finish()
        cA = self.coef[:, l, 3 * k + 0, :]
        cB = self.coef[:, l, 3 * k + 1, :]
        for c in range(NC_):
            ti = self.tmp_next % 2
            self.tmp_next += 1
            self.tt(self.tmp[:, ti, :], self.ob[:, c, :], self.rstd[:], ALU.mult,
                    (self.ob_res[c], self.rstd_res), (self.tmp_res[ti],))
            self.act(self.hT[:, c, :], self.tmp[:, ti, :], AF.Identity, (self.tmp_res[ti], self.coef_res[l]),
                     (self.hT_res,), bias=cB[:, c:c + 1], scale=cA[:, c:c + 1])

    def ffn(self, l, i):
        P = self.P
        wgu = self.ffn_gu[l, i]
        wdn = self.ffn_dn[l, i]
        NG = DFF // 512

        def gate_up(grp):
            hb = grp % 2
            f0 = grp * 512
            for half in range(2):
                fa = f0 + half * 256
                gt, gres = self.ring_load(wgu[:, fa:fa + 256], 16, 256)
                ut, ures = self.ring_load(wgu[:, DFF + fa:DFF + fa + 256], 16, 256)
                for j in range(2):
                    ffc = half * 2 + j
                    for t in range(2):
                        b1 = self.proj_bank()
                        for kc in range(16):
                            self.mm(self.ps[:, b1, :], gt[:, kc, j * 128:(j + 1) * 128], self.hT[:, kc, t * 512:(t + 1) * 512],
                                    kc == 0, kc == 15, (gres, self.hT_res), (self.ps_res[b1],), kc == 15)
                        si = self.sg_next % 3
                        self.sg_next += 1
                        self.act(self.sg[:, si, :], self.ps[:, b1, :], AF.Silu, (self.ps_res[b1],), (self.sg_res[si],))
                        b2 = self.proj_bank()
                        for kc in range(16):
                            self.mm(self.ps[:, b2, :], ut[:, kc, j * 128:(j + 1) * 128], self.hT[:, kc, t * 512:(t + 1) * 512],
                                    kc == 0, kc == 15, (ures, self.hT_res), (self.ps_res[b2],), kc == 15)
                        self.tt(self.hid[:, hb, ffc, t * 512:(t + 1) * 512], self.ps[:, b2, :], self.sg[:, si, :], ALU.mult,
                                (self.ps_res[b2], self.sg_res[si]), (self.hid_res[hb],))

        def down(grp):
            self.down_banks = (4, 5, 0, 1) if grp == NG - 1 else (4, 5, 6, 7)
            hb = grp % 2
            f0 = grp * 512
            dA, dAres = self.ring_load(wdn[f0:f0 + 256, :], 2, D)
            dB, dBres = self.ring_load(wdn[f0 + 256:f0 + 512, :], 2, D)
            for n in range(NC_):
                for t in range(2):
                    b = self.down_bank()
                    for ffc in range(4):
                        wt, wres = (dA, dAres) if ffc < 2 else (dB, dBres)
                        self.mm(self.ps[:, b, :], wt[:, ffc % 2, n * 128:(n + 1) * 128], self.hid[:, hb, ffc, t * 512:(t + 1) * 512],
                                ffc == 0, ffc == 3, (wres, self.hid_res[hb]), (self.ps_res[b],), ffc == 3)
                    o = self.ob[:, n, t * 512:(t + 1) * 512]
                    if grp == 0:
                        self.P.op("act", lambda e, o=o, i=self.ps[:, b, :]: e.copy(o, i), (self.ps_res[b],), (self.ob_res[n],))
                    else:
                        self.tt(o, self.ps[:, b, :], o, ALU.add, (self.ps_res[b], self.ob_res[n]), (self.ob_res[n],))
                if grp == NG - 1:
                    self.stats_acc(n)

        gate_up(0)
        for grp in range(NG):
            if grp + 1 < NG:
                gate_up(grp + 1)
            self.mod_pump(self.mod_rate[(l, i)][grp] if (l, i) in self.mod_rate else 0)
            down(grp)

    def evac(self, out, in_, reads, writes):
        self.evac_next = getattr(self, "evac_next", 0) + 1
        if self.evac_next % 2:
            self.P.op("act", lambda e: e.copy(out, in_), reads, writes)
        else:
            self.P.op("dve", lambda e: e.tensor_copy(out, in_), reads, writes)

    def fourier(self):
        P = self.P
        P.barrier()
        ccv = self.U[:, 0:2048].bitcast(BF16).rearrange("p (a c n) -> p a c n", a=2, c=4)
        cc_res = self.hid_res[0]
        for cs in range(2):
            P.dma("pool", lambda e, o=ccv[:, cs], s=self.ccsc[cs].rearrange("(c p) n -> p c n", p=128): e.dma_start(out=o, in_=s),
                  (), (cc_res,))
        ctv = []
        for cs in range(2):
            v = self.ring[:, 2 * cs:2 * cs + 2, :].rearrange("p s n -> p (s n)").rearrange("p (c n) -> p c n", c=8)
            rs = (self.ring_res[2 * cs], self.ring_res[2 * cs + 1])
            P.dma("pool", lambda e, o=v, s=self.ctst[cs].rearrange("(c p) n -> p c n", p=128): e.dma_start(out=o, in_=s),
                  (), rs)
            ctv.append((v, rs))
        obb = self.ob[:].rearrange("p c t -> p (c t)").bitcast(BF16)

        def Y(g, cs):
            i = g * 2 + cs
            return (obb[:, i * 4096:(i + 1) * 4096].rearrange("p (a n) -> p a n", a=8),
                    (self.ob_res[2 * i], self.ob_res[2 * i + 1]))

        self.proj_nbanks = 8
        for g in range(4):
            for cs in range(2):
                yv, yres = Y(g, cs)
                for tch in range(8):
                    b = self.proj_bank()
                    for cc in range(4):
                        self.mm(self.ps[:, b, :], self.hT[:, 4 * g + cc, tch * 128:(tch + 1) * 128], ccv[:, cs, cc, :],
                                cc == 0, cc == 3, (self.hT_res, cc_res), (self.ps_res[b],), cc == 3)
                    self.evac(yv[:, tch, :], self.ps[:, b, :], (self.ps_res[b],), yres)
        for g in range(4):
            for cq in range(4):
                for tt in range(2):
                    b = self.proj_bank()
                    n = 0
                    for cs in range(2):
                        yv, yres = Y(g, cs)
                        cv, cres = ctv[cs]
                        for tch in range(8):
                            self.mm(self.ps[:, b, :], yv[:, tch, cq * 128:(cq + 1) * 128], cv[:, tch, tt * 512:(tt + 1) * 512],
                                    n == 0, n == 15, yres + cres, (self.ps_res[b],), n == 15)
                            n += 1
                    self.evac(self.hT[:, 4 * g + cq, tt * 512:(tt + 1) * 512], self.ps[:, b, :], (self.ps_res[b],), (self.hT_res,))
        self.proj_nbanks = 6
        self.project_fm(self.fw, lambda n: (self.ob[:, n, :], self.ob_res[n]), done=self.stats_acc)
        self.proj_nbanks = 4
        P.barrier()

    def project_fm(self, W, dst, src=None, src_res=None, done=None):
        if src is None:
            src, src_res = self.hT, self.hT_res
        for nt in range(8):
            wt, wres = self.ring_load(W[:, nt * 256:(nt + 1) * 256], 16, 256)
            for j in range(2):
                n = nt * 2 + j
                o, ores = dst(n)
                for t in range(2):
                    b = self.proj_bank()
                    for kc in range(16):
                        self.mm(self.ps[:, b, :], wt[:, kc, j * 128:(j + 1) * 128], src[:, kc, t * 512:(t + 1) * 512],
                                kc == 0, kc == 15, (wres, src_res), (self.ps_res[b],), kc == 15)
                    self.evac(o[:, t * 512:(t + 1) * 512], self.ps[:, b, :], (self.ps_res[b],), (ores,))
                if done is not None:
                    done(n)

    def gla(self):
        P = self.P
        P.barrier()
        self.ring_nslot = 3
        self.ring_next = 0
        SC = 256.0 ** -0.5
        U = self.U
        qT = U[:, 0:1024].bitcast(BF16).rearrange("p (e t) -> p e t", e=2)
        kT = U[:, 1024:2048].bitcast(BF16).rearrange("p (e t) -> p e t", e=2)
        khT = U[:, 2048:3072].bitcast(BF16).rearrange("p (e t) -> p e t", e=2)
        khat = U[:, 3072:4096].bitcast(BF16).rearrange("p (c k) -> p c k", c=8)
        cs_sb = U[:, 4096:6144].rearrange("p (a t) -> p a t", a=2)
        lrT = U[:, 6144:7168]
        gn_sb = U[:, 7168:7680]
        t12 = U[:, 7680:8704].rearrange("p (a n) -> p a n", a=2)
        R2 = self.ring[:, 3:6, :].rearrange("p s n -> p (s n)").bitcast(F32)
        S_f32 = R2[:, 0:1024].rearrange("p (e n) -> p e n", e=2)
        S_exit = R2[:, 1024:2048].rearrange("p (e n) -> p e n", e=2)
        S_bf = R2[:, 2048:3072].bitcast(BF16).rearrange("p (b e n) -> p b e n", b=2, e=2)
        gc_sb = R2[:, 3072:4480]
        masks = gc_sb[:, 0:256].rearrange("p (a n) -> p a n", a=2)
        ident_f = gc_sb[:, 256:384]
        cmask = gc_sb[:, 384:1408]
        wg_sb = R2[:, 4480:4992].rearrange("p (b n) -> p b n", b=2)
        PT = R2[:, 4992:5120].bitcast(BF16).rearrange("p (b n) -> p b n", b=2)
        X = R2[:, 5120:6144].rearrange("p (e n) -> p e n", e=2)
        on_bf = X[:, 0, :].bitcast(BF16).rearrange("p (b n) -> p b n", b=2)
        flags_sb, hstat, decay, Ltot, ident_bf = self.g_flags, self.g_hstat, self.g_decay, self.g_ltot, self.g_ident
        l1 = self.xin
        L = self.tmp
        rq = self.rstd[:].rearrange("p (e n) -> p e n", e=2)
        rk = self.sq[:].rearrange("p a t -> p (a t)").bitcast(F32).rearrange("p (e n) -> p e n", e=2)
        obb = self.ob[:].rearrange("p c t -> p (c t)").bitcast(BF16)
        oacc = self.ob[:, 0:4, :].rearrange("p c t -> p (c t)").rearrange("p (c n) -> p c n", c=8)
        v_tm = obb[:, 8192:12288].rearrange("p (c n) -> p c n", c=8)
        sr_tm = obb[:, 12288:16384].rearrange("p (c n) -> p c n", c=8)
        OT = obb[:, 16384:32768].rearrange("p (c t) -> p c t", c=16)
        hTf = self.hT[:].rearrange("p c t -> p (c t)").bitcast(F32).rearrange("p (c t) -> p c t", c=8)

        R = lambda n: Res(n)
        r_q, r_k, r_kh, r_khat, r_cs, r_lr, r_gn, r_t12 = R("qT"), R("kT"), R("khT"), R("khat"), R("cs"), R("lrT"), R("gn"), [R("t12a"), R("t12b")]
        r_Sf, r_Se, r_Sb, r_gc, r_wg, r_PT, r_X = R("Sf"), R("Se"), [R("Sb0"), R("Sb1")], R("gc"), [R("wg0"), R("wg1")], [R("PT0"), R("PT1")], R("X")
        r_fl, r_hs, r_dec, r_lt, r_id = R("flags"), R("hstat"), R("decay"), R("ltot"), R("ident")
        r_l1, r_L, r_rq, r_rk = R("l1"), R("L"), R("rq"), R("rk")
        r_S = [R("S0"), R("S1"), R("S2"), R("S3")]
        r_dcp = R("decp")
        decp = self.g_decp
        r_oacc, r_v, r_sr, r_OT, r_on = R("oacc"), R("v"), R("sr"), R("OT"), [R("on0"), R("on1")]
        hres = self.hT_res

        def sp_load(out, in_, res):
            P.dma("sp", lambda e: e.dma_start(out=out, in_=in_), (), (res,))

        sp_load(cs_sb[:, 0, :], self.ropecs[0], r_cs)
        sp_load(cs_sb[:, 1, :], self.ropecs[1], r_cs)
        sp_load(gn_sb, self.gnorm, r_gn)
        sp_load(gc_sb, self.gconst, r_gc)
        sp_load(flags_sb[:], self.flags, r_fl)
        P.op("act", lambda e: e.copy(ident_bf[:], ident_f), (r_gc,), (r_id,))
        P.op("dve", lambda e: e.memset(lrT, 1.0), (), (r_lr,))
        wt, wres = self.ring_load(self.win[:, 8192:8224], 16, 32)
        for t in range(2):
            b = self.proj_bank()
            for kc in range(16):
                self.mm(self.ps[0:32, b, :], wt[:, kc, :], self.hT[:, kc, t * 512:(t + 1) * 512], kc == 0, kc == 15,
                        (wres, hres), (self.ps_res[b],), kc == 15)
            P.op("dve", lambda e, o=lrT[0:32, t * 512:(t + 1) * 512], i=self.ps[0:32, b, :]: e.tensor_copy(o, i),
                 (self.ps_res[b],), (r_lr,))

        wgi = 0
        pti = 0
        t12i = 0
        oni = 0
        for hd in range(4):
            base = hd * 2048
            def v_tiles(h_):
                c0 = h_ * 2048 + 1024
                return (self.ring_load(self.win[:, c0:c0 + 256], 16, 256), self.ring_load(self.win[:, c0 + 256:c0 + 512], 16, 256))

            def v_unit(tiles, tch):
                b = self.proj_bank()
                for half, (tl, rl) in enumerate(tiles):
                    for kc in range(16):
                        self.mm(self.ps[:, b, half * 256:(half + 1) * 256], self.hT[:, kc, tch * 128:(tch + 1) * 128], tl[:, kc, :],
                                kc == 0, kc == 15, (rl, hres), (self.ps_res[b],), kc == 15)
                self.evac(v_tm[:, tch, :], self.ps[:, b, :], (self.ps_res[b],), (r_v,))

            if hd == 0:
                vt = v_tiles(0)
                for tch in range(8):
                    v_unit(vt, tch)
            for d in range(2):
                rt, rrt = self.ring_load(self.win[:, base + 1536 + d * 256:base + 1536 + d * 256 + 256], 16, 256)
                r_units = list(range(8))

                def r_unit(tch, d=d, rt=rt, rrt=rrt):
                    b = self.proj_bank()
                    for kc in range(16):
                        self.mm(self.ps[:, b, 0:256], self.hT[:, kc, tch * 128:(tch + 1) * 128], rt[:, kc, :],
                                kc == 0, kc == 15, (rrt, hres), (self.ps_res[b],), kc == 15)
                    self.evac(sr_tm[:, tch, d * 256:(d + 1) * 256], self.ps[:, b, 0:256], (self.ps_res[b],), (r_sr,))

                wq, rwq = self.ring_load(self.win[:, base + d * 512:base + d * 512 + 256], 16, 256)
                wk, rwk = self.ring_load(self.win[:, base + d * 512 + 256:base + d * 512 + 512], 16, 256)
                bi = wgi % 2
                wgi += 1
                sp_load(wg_sb[0:33, bi, :], self.wg[:, d * 1024 + hd * 256:d * 1024 + hd * 256 + 256], r_wg[bi])
                self.proj_nbanks = 8
                for e_ in range(2):
                    for t in range(2):
                        b = self.proj_bank()
                        self.mm(self.ps[:, b, :], wg_sb[0:33, bi, e_ * 128:(e_ + 1) * 128], lrT[0:33, t * 512:(t + 1) * 512], True, True,
                                (r_wg[bi], r_lr), (self.ps_res[b],), True)
                        ti = t12i % 2
                        t12i += 1
                        self.act(t12[:, ti, :], self.ps[:, b, :], AF.Exp, (self.ps_res[b],), (r_t12[ti],), scale=-1.0)
                        self.act(l1[:, e_, t * 512:(t + 1) * 512], t12[:, ti, :], AF.Ln, (r_t12[ti], self.onef_res), (r_l1, r_S[2], r_S[3]),
                                 bias=self.onef[:, 0:1])
                for e_ in range(2):
                    P.op("dve", lambda e, o=L[:, e_, :], m=cmask, x=l1[:, e_, :]: e.tensor_tensor_scan(o, m, x, 0.0, ALU.mult, ALU.add),
                         (r_gc, r_l1), (r_L,))
                    lv = L[:, e_, :].rearrange("p (c j) -> p c j", j=128)
                    P.op("act", lambda e, o=Ltot[:, e_, :], i=lv[:, :, 127]: e.copy(o, i), (r_L,), (r_lt,))
                self.act(decay[:].rearrange("p e c -> p (e c)"), Ltot[:].rearrange("p e c -> p (e c)"), AF.Exp, (r_lt,), (r_dec,),
                         scale=-1.0 / 16.0)
                if d == 1:
                    for e_ in range(2):
                        lv = L[:, e_, :].rearrange("p (c j) -> p c j", j=128)
                        lb = Ltot[:, e_, :].unsqueeze(2).to_broadcast([128, 8, 128])
                        self.tt(lv, lb, lv, ALU.subtract, (r_lt, r_L), (r_L,))
                        self.tt(L[:, e_, :], L[:, e_, :], l1[:, e_, :], ALU.add, (r_L, r_l1), (r_L,))
                for tt_ in range(2):
                    ts = slice(tt_ * 512, (tt_ + 1) * 512)
                    for wtile, rw, dst, rdst in ((wq, rwq, rq, r_rq), (wk, rwk, rk, r_rk)):
                        bE = self.proj_bank()
                        for kc in range(16):
                            self.mm(self.ps[:, bE, :], wtile[:, kc, 0:128], self.hT[:, kc, ts], kc == 0, kc == 15,
                                    (rw, hres), (self.ps_res[bE],), kc == 15)
                        bO = self.proj_bank()
                        for kc in range(16):
                            self.mm(self.ps[:, bO, :], wtile[:, kc, 128:256], self.hT[:, kc, ts], kc == 0, kc == 15,
                                    (rw, hres), (self.ps_res[bO],), kc == 15)
                        pE, pO = self.ps[:, bE, :], self.ps[:, bO, :]
                        rE, rO = (self.ps_res[bE],), (self.ps_res[bO],)
                        cos, sin = cs_sb[:, 0, ts], cs_sb[:, 1, ts]
                        self.tt(dst[:, 0, :], pE, cos, ALU.mult, rE + (r_cs,), (rdst,))
                        self.tt(dst[:, 1, :], pE, sin, ALU.mult, rE + (r_cs,), (rdst,))
                        self.tt(t12[:, 0, :], pO, sin, ALU.mult, rO + (r_cs,), (r_t12[0],))
                        self.tt(t12[:, 1, :], pO, cos, ALU.mult, rO + (r_cs,), (r_t12[1],))
                        self.tt(dst[:, 0, :], dst[:, 0, :], t12[:, 0, :], ALU.subtract, (rdst, r_t12[0]), (rdst,))
                        self.tt(dst[:, 1, :], dst[:, 1, :], t12[:, 1, :], ALU.add, (rdst, r_t12[1]), (rdst,))
                        for _ in range(2):
                            if r_units:
                                r_unit(r_units.pop(0))
                    self.act(X, L[:, :, ts], AF.Exp, (r_L,), (r_X,), scale=-1.0 / 16.0)
                    self.stt(qT[:, :, ts], rq, SC, X, ALU.mult, ALU.mult, (r_rq, r_X), (r_q,))
                    self.act(X, L[:, :, ts], AF.Exp, (r_L,), (r_X,), scale=1.0 / 16.0)
                    self.tt(rk, rk, X, ALU.mult, (r_rk, r_X), (r_rk,))
                    P.op("act", lambda e, o=kT[:, :, ts], i=rk: e.copy(o, i), (r_rk,), (r_k,))
                    for e_ in range(2):
                        db = decay[:, e_, tt_ * 4:(tt_ + 1) * 4].unsqueeze(2).to_broadcast([128, 4, 128])
                        self.tt(khT[:, e_, ts].rearrange("p (c j) -> p c j", j=128), rk[:, e_, :].rearrange("p (c j) -> p c j", j=128),
                                db, ALU.mult, (r_rk, r_dec), (r_kh,))
                self.proj_nbanks = 4
                for e_ in range(2):
                    b = self.proj_bank()
                    psb = self.ps[:, b, :].bitcast(BF16).rearrange("p (c k) -> p c k", c=8)
                    for tch in range(8):
                        P.op("pe", lambda e, o=psb[:, tch, :], i=khT[:, e_, tch * 128:(tch + 1) * 128]: e.transpose(o, i, ident_bf[:]),
                             (r_kh, r_id), (self.ps_res[b],), tch == 7)
                    self.evac(khat[:, :, e_ * 128:(e_ + 1) * 128], psb, (self.ps_res[b],), (r_khat,))
                fcol = 8 if d == 0 else 16
                for e_ in range(2):
                    self.tt(decp[:, e_, :], decay[:, e_, :], flags_sb[:, fcol:fcol + 8], ALU.mult, (r_dec, r_fl), (r_dcp,))
                PTa = khT[:].rearrange("p e t -> p (e t)")[:, 0:1024].rearrange("p (c j) -> p c j", c=8)
                for g in range(2):
                    b = self.proj_bank()
                    for cc in range(4):
                        c = g * 4 + cc
                        cs_ = slice(c * 128, (c + 1) * 128)
                        for e_ in range(2):
                            self.mm(self.ps[:, b, cc * 128:(cc + 1) * 128], kT[:, e_, cs_], qT[:, e_, cs_], e_ == 0, e_ == 1,
                                    (r_k, r_q, r_khat), (self.ps_res[b],), e_ == 1)
                    mb = masks[:, d, :].unsqueeze(1).to_broadcast([128, 4, 128])
                    self.tt(PTa[:, g * 4:(g + 1) * 4, :], self.ps[:, b, :].rearrange("p (c j) -> p c j", c=4), mb, ALU.mult,
                            (self.ps_res[b], r_gc), (r_kh,))
                sbufs = [S_f32, S_exit, l1[:, 0, :].rearrange("p (e n) -> p e n", e=2), l1[:, 1, :].rearrange("p (e n) -> p e n", e=2)]
                sres = [r_S[0], r_S[1], r_S[2], r_S[3]]
                sext = [(), (), (r_l1,), (r_l1,)]
                P.dma("sp", lambda e, o=sbufs[3], i=self.sinit[d, hd].rearrange("e p n -> p e n"): e.dma_start(out=o, in_=i),
                      (), (sres[3], r_l1))
                P.op("dve", lambda e, o=S_bf[:, 0], i=sbufs[3]: e.tensor_copy(o, i), (sres[3],), (r_Sb[0],))

                def sn(s):
                    c = s if d == 0 else 7 - s
                    for e_ in range(2):
                        bk = 4 + 2 * (s % 2) + e_
                        self.mm(self.ps[:, bk, :], khat[:, c, e_ * 128:(e_ + 1) * 128], v_tm[:, c, :], True, True,
                                (r_khat, r_v), (self.ps_res[bk],), True)

                sn(0)
                sn(1)
                pending_add = None
                for s in range(8):
                    c = s if d == 0 else 7 - s
                    cs_ = slice(c * 128, (c + 1) * 128)
                    cur, prv = s % 4, (s + 3) % 4
                    sb = s % 2
                    for e_ in range(2):
                        bk = 4 + 2 * (s % 2) + e_
                        self.stt(sbufs[cur][:, e_, :], sbufs[prv][:, e_, :], decp[:, e_, c:c + 1], self.ps[:, bk, :], ALU.mult, ALU.add,
                                 (sres[prv], r_dcp, self.ps_res[bk]), (sres[cur],) + sext[cur])
                    if pending_add is not None:
                        pending_add()
                        pending_add = None
                    bo = self.proj_bank()
                    self.mm(self.ps[:, bo, :], PTa[:, c, :], v_tm[:, c, :], True, False, (r_kh, r_v), (self.ps_res[bo],), False)
                    for e_ in range(2):
                        self.mm(self.ps[:, bo, :], qT[:, e_, cs_], S_bf[:, sb, e_, :], False, e_ == 1, (r_q, r_Sb[sb]), (self.ps_res[bo],), e_ == 1)
                    if s + 2 < 8:
                        sn(s + 2)
                    if s < 7:
                        self.act(S_bf[:, 1 - sb].rearrange("p e n -> p (e n)"), sbufs[cur].rearrange("p e n -> p (e n)"), AF.Copy,
                                 (sres[cur], r_fl), (r_Sb[1 - sb],), scale=flags_sb[:, s:s + 1])
                    if d == 0:
                        P.op("act", lambda e, o=oacc[:, c, :], i=self.ps[:, bo, :]: e.copy(o, i), (self.ps_res[bo],), (r_oacc,))
                    else:
                        def _add(o=oacc[:, c, :], i=self.ps[:, bo, :], r=self.ps_res[bo]):
                            self.tt(o, i, o, ALU.add, (r, r_oacc), (r_oacc,))
                        pending_add = _add
                    if s % 2 == 1:
                        q_ = s // 2 if d == 0 else (7 - s) // 2
                        P.dma("sp", lambda e, o=self.st_out[d, hd, q_].rearrange("e p n -> p e n"), i=sbufs[cur]: e.dma_start(out=o, in_=i),
                              (sres[cur],), (), is_output=True)
                if pending_add is not None:
                    pending_add()
            srf = obb[:, 12288:16384]
            self.act(srf, srf, AF.Silu, (r_sr,), (r_sr,))
            for tch in range(8):
                ti = t12i % 2
                t12i += 1
                self.act(t12[:, ti, :], oacc[:, tch, :], AF.Square, (r_oacc,), (r_t12[ti],))
                P.op("dve", lambda e, o=hstat[:, tch:tch + 1], i=t12[:, ti, :]: e.reduce_sum(o, i, mybir.AxisListType.X),
                     (r_t12[ti],), (r_hs,))
            self.act(hstat[:, 8:16], hstat[:, 0:8], AF.Sqrt, (r_hs, self.onef_res), (r_hs,), bias=self.epsb[:, 0:1], scale=1.0 / 512.0)
            P.op("dve", lambda e, o=hstat[:, 8:16], i=hstat[:, 8:16]: e.reciprocal(o, i), (r_hs,), (r_hs,))
            vt = v_tiles(hd + 1) if hd < 3 else None
            for tch in range(8):
                if vt is not None:
                    v_unit(vt, tch)
                ti = t12i % 2
                t12i += 1
                self.stt(t12[:, ti, :], oacc[:, tch, :], hstat[:, 8 + tch:9 + tch], gn_sb, ALU.mult, ALU.mult,
                         (r_oacc, r_hs, r_gn), (r_t12[ti],))
                oi = oni % 2
                oni += 1
                self.tt(on_bf[:, oi, :], t12[:, ti, :], sr_tm[:, tch, :], ALU.mult, (r_t12[ti], r_sr), (r_on[oi], r_X))
                b = self.proj_bank()
                psb = self.ps[:, b, :].bitcast(BF16)[:, 0:512].rearrange("p (c k) -> p c k", c=4)
                for dvc in range(4):
                    P.op("pe", lambda e, o=psb[:, dvc, :], i=on_bf[:, oi, dvc * 128:(dvc + 1) * 128]: e.transpose(o, i, ident_bf[:]),
                         (r_on[oi], r_X, r_id), (self.ps_res[b],), dvc == 3)
                self.evac(OT[:, hd * 4:(hd + 1) * 4, tch * 128:(tch + 1) * 128], psb, (self.ps_res[b],), (r_OT,))
        P.barrier()
        self.proj_nbanks = 8
        self.project_fm(self.wout, lambda n: (self.ob[:, n, :], self.ob_res[n]) if n < 8 else (hTf[:, n - 8, :], self.hT_res),
                        src=OT, src_res=r_OT)
        self.proj_nbanks = 4
        P.barrier()
        for i in range(8):
            self.stats_acc(i)
        for i in range(8):
            self.evac(self.ob[:, 8 + i, :], hTf[:, i, :], (self.hT_res,), (self.ob_res[8 + i],))
            self.stats_acc(8 + i)
        P.barrier()
        self.ring_nslot = NSLOT

    def build(self):
        stages = ["ffn00", "mix0", "ffn01", "ffn10", "mix1", "ffn11"]
        stop = self.stop_after or "ffn11"
        nst = stages.index(stop) + 1
        self.setup()
        x_src, x_res = self.xT, None
        for si in range(nst):
            l, k = si // 3, si % 3
            last = si == nst - 1
            if si == 0:
                self.load_x0()
            self.coef_AB(l, k)
            self.prenorm(l, k)
            if k == 0:
                self.ffn(l, 0)
            elif k == 2:
                self.ffn(l, 1)
            elif l == 0:
                self.fourier()
            else:
                self.gla()
            self.coef_C(l, k)
            if last:
                self.residual(l, k, x_src, x_res, self.yT, None, is_output=True)
            else:
                self.residual(l, k, x_src, x_res, self.xd, self.xd_res)
            x_src, x_res = self.xd, self.xd_res
        self.finish()
        self.P.check_deadlock()
        self.P.emit()
        return self.nc

    def finish(self):
        pass


def _pcol(v):
    v = np.asarray(v, np.float32)
    return np.ascontiguousarray(v.reshape(-1, 128).T)


def _wg_aug(inp):
    kp = _kperm()
    wg = np.zeros((33, 2048), np.float32)
    for d in range(2):
        wg[16 * d:16 * d + 16, d * 1024:(d + 1) * 1024] = inp["gla_w_gate_up"][0, d][:, kp]
        wg[32, d * 1024:(d + 1) * 1024] = inp["gla_b_gate"][0, d][kp]
    return wg


def _core_inputs(core, inp, common):
    if core < 4:
        x = inp["x_prompt"][core * 4:(core + 1) * 4].reshape(T, D)
        cond = inp["c_ctx"]
    else:
        x = inp["x_sample"][core - 4]
        cond = inp["c"][core - 4]
    m = dict(common)
    ccsc, ctst = _dft_consts(256 if core < 4 else 1024)
    m["ccsc"] = ccsc
    m["ctst"] = ctst
    m["ropecs"] = _rope_tables(core >= 4)
    if core >= 4:
        s0 = inp["state_gla"][core - 4, 0]
        m["sinit"] = np.ascontiguousarray(s0.reshape(2, 4, 128, 2, 512).transpose(0, 1, 3, 2, 4), dtype=np.float32)
        fl = np.ones(8, np.float32)
    else:
        m["sinit"] = np.zeros((2, 4, 2, 128, 512), np.float32)
        fl = np.zeros(8, np.float32)
        fl[0::2] = 1.0
    fprev = np.concatenate([[1.0], fl[:7]]).astype(np.float32)
    m["flags"] = np.ascontiguousarray(np.broadcast_to(np.concatenate([fl, fprev, fprev[::-1]])[None, :], (128, 24)), dtype=np.float32)
    m["xT"] = np.ascontiguousarray(x.T)
    m["cond"] = _pcol(cond)
    return m


def _dft_consts(seq_len):
    n = np.arange(512)
    ang = 2.0 * np.pi * ((n[:, None] * n[None, :]) % 512) / 512.0
    ccsc = np.stack([np.cos(ang), np.sin(ang)]) / np.sqrt(512.0)
    t = np.arange(seq_len)
    a2 = 2.0 * np.pi * ((t[:, None] * t[None, :]) % seq_len) / seq_len
    c1, s1 = np.cos(a2) / np.sqrt(seq_len), -np.sin(a2) / np.sqrt(seq_len)
    ct = np.zeros((T, T)); st = np.zeros((T, T))
    for i in range(T // seq_len):
        sl = slice(i * seq_len, (i + 1) * seq_len)
        ct[sl, sl] = c1
        st[sl, sl] = s1
    return ccsc.astype(np.float32), np.stack([ct, st]).astype(np.float32)


def _gla_perm():
    ev = np.arange(0, 256, 2)
    od = np.arange(1, 256, 2)
    cols = []
    for hd in range(4):
        for blk in range(4):
            b0 = blk * 1024 + hd * 256
            cols.append(b0 + ev)
            cols.append(b0 + od)
        cols.append(4096 + hd * 512 + np.arange(512))
        cols.append(6144 + hd * 512 + np.arange(512))
    cols.append(np.arange(8192, 8224))
    return np.concatenate(cols)


def _kperm():
    ev = np.arange(0, 256, 2)
    od = np.arange(1, 256, 2)
    return np.concatenate([hd * 256 + np.concatenate([ev, od]) for hd in range(4)])


def _gla_consts():
    j = np.arange(128)
    mf = (j[:, None] <= j[None, :]).astype(np.float32)
    mb = (j[:, None] >= j[None, :]).astype(np.float32)
    ident = np.eye(128, dtype=np.float32)
    cm = np.ones((128, T), np.float32)
    cm[:, ::128] = 0.0
    return np.ascontiguousarray(np.concatenate([mf, mb, ident, cm], axis=1))


def _rope_tables(is_sample):
    if not is_sample:
        return np.stack([np.ones((128, T), np.float32), np.zeros((128, T), np.float32)])
    t = np.arange(T)
    row = (t // 64).astype(np.float32)
    col = (t % 64).astype(np.float32)
    inv = (np.float32(10000.0) ** (-np.arange(64, dtype=np.float32) / np.float32(64.0))).astype(np.float32)
    ang = np.concatenate([row[:, None] * inv[None, :], col[:, None] * inv[None, :]], axis=1)
    return np.stack([np.cos(ang).T, np.sin(ang).T]).astype(np.float32)


def _common_inputs(inp):
    nv = np.zeros((128, 193), np.float32)
    nv[0, 192] = nv[32, 192] = nv[64, 192] = 1.0
    for l in range(2):
        for k in range(3):
            nv[:, ((l * 3 + k) * 2 + 0) * 16:((l * 3 + k) * 2 + 0) * 16 + 16] = _pcol(inp["norm_pre"][l, k])
            nv[:, ((l * 3 + k) * 2 + 1) * 16:((l * 3 + k) * 2 + 1) * 16 + 16] = _pcol(inp["norm_post"][l, k])
    ab = np.concatenate([_pcol(inp["ada_b"][0]), _pcol(inp["ada_b"][1])], axis=1)
    return {
        "ada_w": np.ascontiguousarray(inp["ada_w"], dtype=np.float32),
        "ada_bT": np.ascontiguousarray(ab),
        "nvec": nv,
        "ffn_gu": np.ascontiguousarray(inp["ffn_w_gate_up"], dtype=np.float32),
        "ffn_dn": np.ascontiguousarray(inp["ffn_w_down"], dtype=np.float32),
        "fw": np.ascontiguousarray(inp["fourier_w"][0], dtype=np.float32),
        "win": np.ascontiguousarray(inp["gla_w_in"][0][:, _gla_perm()], dtype=np.float32),
        "wg": _wg_aug(inp),
        "gnorm": np.ascontiguousarray(np.broadcast_to(inp["gla_norm"][0][None, :], (128, 512)), dtype=np.float32),
        "wout": np.ascontiguousarray(inp["gla_w_out"][0], dtype=np.float32),
        "gconst": _gla_consts(),
    }


def _build_cached(stop_after=None):
    b = Builder(stop_after=stop_after)
    return b.build()


def run_partial(inp, stop_after, cores=(0, 4), trace=False):
    nc = _build_cached(stop_after)
    common = _common_inputs(inp)
    in_maps = [_core_inputs(c, inp, common) for c in cores]
    names = set()
    for alloc in nc.allocations:
        try:
            if alloc.kind == "ExternalInput":
                names.add(alloc.memorylocations[0].name)
        except Exception:
            pass
    if names:
        in_maps = [{k: v for k, v in m.items() if k in names} for m in in_maps]
    res = run_bass_kernel_spmd(nc, in_maps, core_ids=list(range(len(cores))), trace=trace)
    return res


def kernel(**inputs):
    inp = {k: np.asarray(v) for k, v in inputs.items()}
    res = run_partial(inp, None, cores=tuple(range(NCORES)))
    y_prompt = np.empty((16, 256, D), np.float32)
    y_sample = np.empty((2, T, D), np.float32)
    new_state = np.empty((16, 1, 2, 4, 256, 512), np.float32)
    for c in range(NCORES):
        r = res.results[c]
        y = np.asarray(r["yT"]).T
        if c < 4:
            y_prompt[c * 4:(c + 1) * 4] = y.reshape(4, 256, D)
            st = np.asarray(r["st_out"])
            st = st.transpose(2, 0, 1, 4, 3, 5).reshape(4, 2, 4, 256, 512)
            new_state[c * 4:(c + 1) * 4, 0] = st
        else:
            y_sample[c - 4] = y
    return (y_prompt, y_sample, new_state)
```
